# Optimizing a Trainium2 kernel written in Bass

```python
import math
import jax, jax.numpy as jnp
from jax import lax
import numpy as np

D_MODEL = 1024
BATCH = 16
SEQ = 2048
DEPTH = 1
DEC_BATCH = 32
DEC_SEQ = 4
PAST_LEN = 16384
PAGE_SIZE = 128

D_ATT = D_MODEL // 2
N_HEADS_ATT = 8
HEAD_DIM_ATT = D_ATT // N_HEADS_ATT
D_MLSTM = D_MODEL - D_ATT
N_HEADS_MLSTM = 4
HEAD_DIM_MLSTM = D_MLSTM // N_HEADS_MLSTM
D_MIX = D_ATT + D_MLSTM
ROT_DIM = HEAD_DIM_ATT // 4
ROPE_THETA = 500000.0
DILATED_PATTERNS = ((128, 1), (512, 4), (2048, 16))
MAX_WINDOW = 2048
ATT_BLOCK = 128
CONV_WIDTH = 4
MLSTM_CHUNK = 128
EPS = 1e-6
NEG_INF = -1e30
SPLIT_SIZES = (D_ATT, D_ATT, D_ATT, D_ATT,
               D_MLSTM, D_MLSTM, D_MLSTM, D_MLSTM, D_MLSTM, N_HEADS_MLSTM, N_HEADS_MLSTM)
N_IN = sum(SPLIT_SIZES)

kernel_name = 'hymba_dilated_attn_mlstm_decode_step'


def rmsnorm(x, g):
    xf = x.astype(jnp.float32)
    y = xf * lax.rsqrt(jnp.mean(xf * xf, axis=-1, keepdims=True) + EPS)
    return (y * g.astype(jnp.float32)).astype(x.dtype)


def rope_partial(x, pos):
    half = ROT_DIM // 2
    inv = ROPE_THETA ** (-jnp.arange(half, dtype=jnp.float32) * 2.0 / ROT_DIM)
    ang = pos.astype(jnp.float32)[:, None] * inv[None, :]
    cos = jnp.cos(ang)[None, :, None, :]
    sin = jnp.sin(ang)[None, :, None, :]
    x1 = x[..., :half]
    x2 = x[..., half:ROT_DIM]
    return jnp.concatenate([x1 * cos - x2 * sin, x2 * cos + x1 * sin, x[..., ROT_DIM:]], axis=-1)


def dilated_partial_prompt(q, k, v, dil, n_back):
    B, S, H, D = q.shape
    L = S // dil
    nb = -(-L // ATT_BLOCK)
    Lp = nb * ATT_BLOCK

    def to_sub(t):
        t = t.reshape(B, L, dil, H, D).transpose(0, 2, 1, 3, 4)
        t = jnp.pad(t, ((0, 0), (0, 0), (0, Lp - L), (0, 0), (0, 0)))
        return t.reshape(B, dil, nb, ATT_BLOCK, H, D)

    qs, ks, vs = to_sub(q), to_sub(k), to_sub(v)

    def with_prev(t):
        prev = jnp.concatenate([jnp.zeros_like(t[:, :, :1]), t[:, :, :-1]], axis=2)
        return jnp.concatenate([prev, t], axis=3)

    kk, vv = with_prev(ks), with_prev(vs)
    s = jnp.einsum('brcqhe,brckhe->brchqk', qs, kk) * (HEAD_DIM_ATT ** -0.5)
    u = jnp.arange(ATT_BLOCK)[:, None]
    wk = jnp.arange(2 * ATT_BLOCK)[None, :]
    dist = u + ATT_BLOCK - wk
    in_win = (dist >= 0) & (dist <= n_back)
    has_prev = (jnp.arange(nb) > 0)[:, None, None] | (wk >= ATT_BLOCK)[None]
    valid = in_win[None] & has_prev
    s = jnp.where(valid[None, None, :, None], s, NEG_INF)
    m = jnp.max(s, axis=-1)
    p = jnp.exp(s - m[..., None])
    den = jnp.sum(p, axis=-1)
    acc = jnp.einsum('brchqk,brckhe->brcqhe', p, vv)

    def stat_back(t):
        t = t.transpose(0, 1, 2, 4, 3).reshape(B, dil, Lp, H)[:, :, :L]
        return t.transpose(0, 2, 1, 3).reshape(B, S, H)

    acc = acc.reshape(B, dil, Lp, H, D)[:, :, :L].transpose(0, 2, 1, 3, 4).reshape(B, S, H, D)
    return stat_back(m), stat_back(den), acc


def dilated_partial_sample(q, kc, vc, dil, n_back):
    T = q.shape[1]
    WB = kc.shape[1] - T
    j = jnp.arange(n_back + 1)
    idx = WB + jnp.arange(T)[:, None] - j[None, :] * dil
    valid = idx >= 0
    idx = jnp.maximum(idx, 0)
    kg = jnp.take(kc, idx, axis=1)
    vg = jnp.take(vc, idx, axis=1)
    s = jnp.einsum('bthe,btjhe->bthj', q, kg) * (HEAD_DIM_ATT ** -0.5)
    s = jnp.where(valid[None, :, None, :], s, NEG_INF)
    m = jnp.max(s, axis=-1)
    p = jnp.exp(s - m[..., None])
    den = jnp.sum(p, axis=-1)
    acc = jnp.einsum('bthj,btjhe->bthe', p, vg)
    return m, den, acc


def combine_partials(parts):
    ms = jnp.stack([pt[0] for pt in parts])
    ss = jnp.stack([pt[1] for pt in parts])
    accs = jnp.stack([pt[2] for pt in parts])
    w = jnp.exp(ms - jnp.max(ms, axis=0, keepdims=True))
    den = jnp.sum(w * ss, axis=0)
    num = jnp.sum(w[..., None] * accs, axis=0)
    return num / den[..., None]


def causal_conv(u, buf, w, b):
    S = u.shape[1]
    xp = jnp.concatenate([buf.astype(u.dtype), u], axis=1)
    y = b + xp[:, 0:S] * w[0]
    for j in range(1, CONV_WIDTH):
        y = y + xp[:, j:j + S] * w[j]
    return jax.nn.silu(y), xp[:, -(CONV_WIDTH - 1):]


def mlstm_chunkwise(q, k, v, i_pre, log_f, C0, n0, m0):
    B, H, S, DK = q.shape
    DV = v.shape[-1]
    L = MLSTM_CHUNK if S % MLSTM_CHUNK == 0 else S
    nc = S // L

    def chunks(t):
        return jnp.moveaxis(t.reshape(t.shape[:2] + (nc, L) + t.shape[3:]), 2, 0)

    causal = jnp.tril(jnp.ones((L, L), dtype=bool))

    def step(carry, xs):
        C, n, m_prev = carry
        qc, kc, vc, ic, fc = xs
        b = jnp.cumsum(fc, axis=-1)
        m_t = jnp.maximum(m_prev[..., None] + b, b + lax.cummax(ic - b, axis=2))
        inter = jnp.exp(m_prev[..., None] + b - m_t)
        dlog = b[..., :, None] - b[..., None, :] + ic[..., None, :] - m_t[..., :, None]
        dmat = jnp.exp(jnp.where(causal, dlog, NEG_INF))
        sqk = jnp.einsum('bhtd,bhsd->bhts', qc, kc) * dmat
        num = inter[..., None] * jnp.einsum('bhtd,bhde->bhte', qc, C) + jnp.einsum('bhts,bhse->bhte', sqk, vc)
        nq = inter * jnp.einsum('bhtd,bhd->bht', qc, n) + jnp.sum(sqk, axis=-1)
        den = jnp.maximum(jnp.abs(nq), jnp.exp(-m_t))
        h = num / den[..., None]
        m_last = m_t[..., -1]
        decay = jnp.exp(m_prev + b[..., -1] - m_last)
        w = jnp.exp(b[..., -1:] - b + ic - m_last[..., None])
        C_new = decay[..., None, None] * C + jnp.einsum('bhs,bhsd,bhse->bhde', w, kc, vc)
        n_new = decay[..., None] * n + jnp.einsum('bhs,bhsd->bhd', w, kc)
        return (C_new, n_new, m_last), h

    xs = (chunks(q), chunks(k), chunks(v), chunks(i_pre), chunks(log_f))
    (C, n, m), hs = lax.scan(step, (C0, n0, m0), xs)
    h = jnp.moveaxis(hs, 0, 2).reshape(B, H, S, DV)
    return h, C, n, m


def mixer_layer(x, pos, win_k, win_v, conv_buf, C0, n0, m0,
                norm_g, w_in, conv_w, conv_b, b_i, b_f, mlstm_norm_g, w_out):
    B, S, _ = x.shape
    f32 = jnp.float32
    hn = rmsnorm(x, norm_g)
    u = hn @ w_in
    offsets = [int(o) for o in np.cumsum(SPLIT_SIZES)[:-1]]
    qa, ka, va, za, qb, kb, vb, ob, zb, ib, fb = jnp.split(u, offsets, axis=-1)

    att_heads = lambda t: t.reshape(B, S, N_HEADS_ATT, HEAD_DIM_ATT).astype(f32)
    qa = rope_partial(att_heads(qa), pos)
    ka = rope_partial(att_heads(ka), pos)
    va = att_heads(va)
    if win_k is None:
        parts = [dilated_partial_prompt(qa, ka, va, dil, win // dil) for (win, dil) in DILATED_PATTERNS]
        keep = min(MAX_WINDOW, S)
        new_k, new_v = ka[:, -keep:], va[:, -keep:]
    else:
        kc = jnp.concatenate([win_k.astype(f32), ka], axis=1)
        vc = jnp.concatenate([win_v.astype(f32), va], axis=1)
        parts = [dilated_partial_sample(qa, kc, vc, dil, win // dil) for (win, dil) in DILATED_PATTERNS]
        keep = win_k.shape[1]
        new_k, new_v = kc[:, -keep:], vc[:, -keep:]
    att = combine_partials(parts).reshape(B, S, D_ATT).astype(x.dtype)
    y_a = att * jax.nn.silu(za)

    qk, new_conv = causal_conv(jnp.concatenate([qb, kb], axis=-1), conv_buf, conv_w, conv_b)
    qm, km = jnp.split(qk, 2, axis=-1)
    m_heads = lambda t: t.reshape(B, S, N_HEADS_MLSTM, HEAD_DIM_MLSTM).transpose(0, 2, 1, 3).astype(f32)
    i_pre = (ib.astype(f32) + b_i.astype(f32)).transpose(0, 2, 1)
    log_f = jax.nn.log_sigmoid(fb.astype(f32) + b_f.astype(f32)).transpose(0, 2, 1)
    h_m, C, n, m = mlstm_chunkwise(m_heads(qm), m_heads(km) * (HEAD_DIM_MLSTM ** -0.5), m_heads(vb),
                                   i_pre, log_f, C0.astype(f32), n0.astype(f32), m0.astype(f32))
    h_m = h_m.transpose(0, 2, 1, 3)
    h_m = jax.nn.sigmoid(ob.astype(f32)).reshape(B, S, N_HEADS_MLSTM, HEAD_DIM_MLSTM) * h_m
    h_m = h_m * lax.rsqrt(jnp.mean(h_m * h_m, axis=-1, keepdims=True) + EPS)
    h_m = h_m * mlstm_norm_g.astype(f32).reshape(N_HEADS_MLSTM, HEAD_DIM_MLSTM)
    y_b = h_m.reshape(B, S, D_MLSTM).astype(x.dtype) * jax.nn.silu(zb)

    y = x + jnp.concatenate([y_a, y_b], axis=-1) @ w_out
    return y, (new_k.astype(x.dtype), new_v.astype(x.dtype), new_conv, C, n, m)


def setup_inputs(seed: int = 0) -> dict:
    key = jax.random.key(seed)
    ks = jax.random.split(key, 20)
    win_buf = min(MAX_WINDOW, PAST_LEN)
    nrm = lambda k, shp: jax.random.normal(k, shp, dtype=jnp.float32)
    return {
        'x_prompt': nrm(ks[0], (BATCH, SEQ, D_MODEL)),
        'x_sample': nrm(ks[1], (DEC_BATCH, DEC_SEQ, D_MODEL)),
        'cache_win_k': nrm(ks[2], (DEPTH, DEC_BATCH, win_buf, N_HEADS_ATT, HEAD_DIM_ATT)),
        'cache_win_v': nrm(ks[3], (DEPTH, DEC_BATCH, win_buf, N_HEADS_ATT, HEAD_DIM_ATT)),
        'state_conv': nrm(ks[4], (DEPTH, DEC_BATCH, CONV_WIDTH - 1, 2 * D_MLSTM)),
        'state_C': 0.1 * nrm(ks[5], (DEPTH, DEC_BATCH, N_HEADS_MLSTM, HEAD_DIM_MLSTM, HEAD_DIM_MLSTM)),
        'state_n': 0.1 * nrm(ks[6], (DEPTH, DEC_BATCH, N_HEADS_MLSTM, HEAD_DIM_MLSTM)),
        'state_m': 0.5 * nrm(ks[7], (DEPTH, DEC_BATCH, N_HEADS_MLSTM)),
        'norm_g': 1.0 + 0.02 * nrm(ks[8], (DEPTH, D_MODEL)),
        'w_in': nrm(ks[9], (DEPTH, D_MODEL, N_IN)) * D_MODEL ** -0.5,
        'conv_w': nrm(ks[10], (DEPTH, CONV_WIDTH, 2 * D_MLSTM)) * CONV_WIDTH ** -0.5,
        'conv_b': 0.02 * nrm(ks[11], (DEPTH, 2 * D_MLSTM)),
        'b_i': 0.1 * nrm(ks[12], (DEPTH, N_HEADS_MLSTM)),
        'b_f': jnp.linspace(3.0, 6.0, N_HEADS_MLSTM, dtype=jnp.float32)[None] + 0.1 * nrm(ks[13], (DEPTH, N_HEADS_MLSTM)),
        'mlstm_norm_g': 1.0 + 0.02 * nrm(ks[14], (DEPTH, D_MLSTM)),
        'w_out': nrm(ks[15], (DEPTH, D_MIX, D_MODEL)) * D_MIX ** -0.5,
        'final_norm_g': 1.0 + 0.02 * nrm(ks[16], (D_MODEL,)),
    }


def reference(x_prompt, x_sample, cache_win_k, cache_win_v, state_conv, state_C, state_n, state_m,
              norm_g, w_in, conv_w, conv_b, b_i, b_f, mlstm_norm_g, w_out, final_norm_g):
    B, S, _ = x_prompt.shape
    DB, T, _ = x_sample.shape
    pos_p = jnp.arange(S, dtype=jnp.int32)
    pos_s = PAST_LEN + jnp.arange(T, dtype=jnp.int32)
    xp, xs = x_prompt, x_sample
    p_states, s_states = [], []
    for l in range(DEPTH):
        params = (norm_g[l], w_in[l], conv_w[l], conv_b[l], b_i[l], b_f[l], mlstm_norm_g[l], w_out[l])
        zero_conv = jnp.zeros((B, CONV_WIDTH - 1, 2 * D_MLSTM), dtype=xp.dtype)
        zero_C = jnp.zeros((B, N_HEADS_MLSTM, HEAD_DIM_MLSTM, HEAD_DIM_MLSTM), dtype=jnp.float32)
        zero_n = jnp.zeros((B, N_HEADS_MLSTM, HEAD_DIM_MLSTM), dtype=jnp.float32)
        zero_m = jnp.zeros((B, N_HEADS_MLSTM), dtype=jnp.float32)
        xp, sp = mixer_layer(xp, pos_p, None, None, zero_conv, zero_C, zero_n, zero_m, *params)
        xs, ss = mixer_layer(xs, pos_s, cache_win_k[l], cache_win_v[l], state_conv[l],
                             state_C[l], state_n[l], state_m[l], *params)
        p_states.append(sp)
        s_states.append(ss)
    y_prompt = rmsnorm(xp, final_norm_g)
    y_sample = rmsnorm(xs, final_norm_g)
    stk = lambda sts, i: jnp.stack([st[i] for st in sts])
    p_k, p_v, p_conv, p_C, p_n, p_m = [stk(p_states, i) for i in range(6)]
    s_k, s_v, s_conv, s_C, s_n, s_m = [stk(s_states, i) for i in range(6)]
    return (y_prompt, y_sample, p_k, p_v, p_conv, p_C, p_n, p_m, s_k, s_v, s_conv, s_C, s_n, s_m)
```

```python
import math
import os
KSTAGE = int(os.environ.get('KSTAGE', '3'))
KTILES = int(os.environ.get('KTILES', '999'))
KSEQS = int(os.environ.get('KSEQS', '99'))
KSUB = int(os.environ.get('KSUB', '99'))
KNOSAMPLE = int(os.environ.get('KNOSAMPLE', '0'))
KONECORE = int(os.environ.get('KONECORE', '0'))
KSB = int(os.environ.get('KSB', '99'))
from contextlib import ExitStack
import numpy as np
import ml_dtypes
import concourse.bass as bass
import concourse.mybir as mybir
from concourse.bass_utils import run_bass_kernel_spmd
from concourse.alu_op_type import AluOpType as ALU

F32 = mybir.dt.float32
BF16 = mybir.dt.bfloat16
AF = mybir.ActivationFunctionType
AX = mybir.AxisListType

NCORES = 8
D = 1024
SEQ = 2048
NSEQ = 2
NSB = 4
TS = 4
WB = 2048
EPS = 1e-6
PAST = 16384
NIN = 4616
WTC = 3592
NDS = 40


class Prog:
    def __init__(self, nc, es):
        self.nc = nc
        self.engs = {'pe': nc.tensor, 'act': nc.scalar, 'dve': nc.vector, 'pool': nc.gpsimd, 'sp': nc.sync}
        self.sem = {k: es.enter_context(nc.semaphore('sem_' + k)) for k in ['pe', 'act', 'dve', 'pool']}
        self.cnt = {k: 0 for k in self.sem}
        self.seen = {k: {} for k in self.engs}
        self.lastw = {}
        self.readers = {}
        self.dsems = [es.enter_context(nc.semaphore(f'dsem{i}')) for i in range(NDS)]
        self.dcnt = [0] * NDS
        self.dnext = 0
        self.rec = None

    def _wait(self, e, dep):
        src, ticket = dep
        if src == e and e == 'pe':
            return
        if self.seen[e].get(src, 0) >= ticket:
            return
        sem = self.sem[src] if isinstance(src, str) else self.dsems[src]
        self.engs[e].wait_ge(sem, ticket)
        self.seen[e][src] = ticket

    def _deps(self, e, r, w):
        deps = []
        for x in r:
            if x in self.lastw:
                deps.append(self.lastw[x])
        for x in w:
            if x in self.lastw:
                deps.append(self.lastw[x])
            for s, tk in self.readers.get(x, {}).items():
                deps.append((s, tk))
        for d in deps:
            self._wait(e, d)

    def _reg(self, tk, r, w):
        for x in r:
            self.readers.setdefault(x, {})[tk[0]] = tk[1]
        for x in w:
            self.lastw[x] = tk
            self.readers[x] = {}

    @staticmethod
    def _excl(r, w):
        w2 = list(w) + [x for x in r if x.startswith('ps')]
        r2 = [x for x in r if not x.startswith('ps')]
        return r2, w2

    def merged(self, segs):
        lists = []
        for seg in segs:
            self.rec = []
            seg()
            lists.append(self.rec)
        self.rec = None
        idx = [0] * len(lists)
        while True:
            best, bf = None, None
            for i, L in enumerate(lists):
                if idx[i] < len(L):
                    f = (idx[i] + 1) / len(L)
                    if bf is None or f < bf:
                        best, bf = i, f
            if best is None:
                break
            kind, args, kw = lists[best][idx[best]]
            idx[best] += 1
            (self.op if kind == 'op' else self.dma)(*args, **kw)

    def op(self, e, fn, r=(), w=()):
        if getattr(self, 'rec', None) is not None:
            self.rec.append(('op', (e, fn, list(r), list(w)), {}))
            return
        r, w = self._excl(r, w)
        self._deps(e, r, w)
        ins = fn(self.engs[e])
        self.cnt[e] += 1
        ins.then_inc(self.sem[e], 1)
        self._reg((e, self.cnt[e]), r, w)

    def dma(self, q, out, in_, r=(), w=(), **kw):
        if getattr(self, 'rec', None) is not None:
            self.rec.append(('dma', (q, out, in_, list(r), list(w)), kw))
            return
        self._deps(q, r, w)
        i = self.dnext
        self.dnext = (i + 1) % NDS
        if self.dcnt[i] > 0:
            self._wait(q, (i, self.dcnt[i]))
        with self.nc.allow_non_contiguous_dma(reason="strided/small layouts"):
            self.engs[q].dma_start(out=out, in_=in_, **kw).then_inc(self.dsems[i], 16)
        self.dcnt[i] += 16
        self._reg((i, self.dcnt[i]), r, w)

    def barrier(self):
        for e in self.engs:
            for s in self.sem:
                if self.cnt[s] > 0:
                    self._wait(e, (s, self.cnt[s]))
            for i in range(NDS):
                if self.dcnt[i] > 0:
                    self._wait(e, (i, self.dcnt[i]))


def build_program():
    nc = bass.Bass("TRN2", target_bir_lowering=False)

    def din(name, shape, dt=F32):
        return nc.dram_tensor(name, list(shape), dt, kind="ExternalInput").ap()

    def dout(name, shape):
        return nc.dram_tensor(name, list(shape), F32, kind="ExternalOutput").ap()

    def dscr(name, shape, dt):
        return nc.dram_tensor(name, list(shape), dt, kind="Internal").ap()

    xp = din("xp", [NSEQ, SEQ, D])
    xs = din("xs", [NSB, TS, D])
    WBD = 8 if KNOSAMPLE else WB
    cache_k = din("cache_k", [NSB, WBD, 512])
    cache_v = din("cache_v", [NSB, WBD, 512])
    st_conv = din("st_conv", [NSB, 3, D])
    st_C = din("st_C", [NSB, 4, 128, 128])
    st_n = din("st_n", [NSB, 4, 128])
    st_m = din("st_m", [NSB, 4])
    norm_g = din("norm_g", [D])
    w_in = din("w_in", [D, NIN])
    conv_w = din("conv_w", [4, D])
    conv_b = din("conv_b", [D])
    b_if = din("b_if", [8])
    gm = din("gm", [512])
    w_out = din("w_out", [D, D])
    gf = din("gf", [D])
    c_identb = din("c_identb", [128, 128], BF16)
    c_identf = din("c_identf", [128, 128])
    c_mask = din("c_mask", [128, 256], BF16)
    c_sel = din("c_sel", [4, 512])
    c_36 = din("c_36", [36, 644], BF16)
    c_bd = din("c_bd", [8, 528])
    c_cosp = din("c_cosp", [SEQ, 64])
    c_sinp = din("c_sinp", [SEQ, 64])
    c_coss = din("c_coss", [TS, 64])
    c_sins = din("c_sins", [TS, 64])

    y_p = dout("y_p", [NSEQ, SEQ, D])
    y_s = dout("y_s", [NSB, TS, D])
    p_k = dout("p_k", [NSEQ, SEQ, 512])
    p_v = dout("p_v", [NSEQ, SEQ, 512])
    p_conv = dout("p_conv", [NSEQ, 3, D])
    p_C = dout("p_C", [NSEQ, 4, 128, 128])
    p_n = dout("p_n", [NSEQ, 4, 128])
    p_m = dout("p_m", [NSEQ, 4])
    s_k = dout("s_k", [NSB, WBD, 512])
    s_v = dout("s_v", [NSB, WBD, 512])
    s_conv = dout("s_conv", [NSB, 3, D])
    s_C = dout("s_C", [NSB, 4, 128, 128])
    s_n = dout("s_n", [NSB, 4, 128])
    s_m = dout("s_m", [NSB, 4])

    vaug_scr = dscr("vaug_scr", [NSEQ, SEQ, 1024], BF16)
    scr_n2 = dscr("scr_n2", [NSEQ, 128, 16], F32)
    scr_mx = dscr("scr_mx", [NSEQ, 16], F32)
    qs_scr = dscr("qs_scr", [NSB, TS, 512], F32)

    es = ExitStack()
    with es:
        P = Prog(nc, es)

        uid = [0]

        def sb(name, shape, dt=F32, stack=es):
            uid[0] += 1
            return stack.enter_context(nc.sbuf_tensor(f"{name}_{uid[0]}", list(shape), dt))

        WT = sb("WT", [128, 8, WTC], BF16)
        WF = sb("WF", [128, 8, 1024], BF16)
        identb = sb("identb", [128, 128], BF16)
        mask = sb("mask", [128, 256], BF16)
        onesf = sb("onesf", [128, 128])
        g8 = sb("g8", [128, 8])
        gf_bc = sb("gf_bc", [128, D])
        gm_bc = sb("gm_bc", [128, 512])
        cw = sb("cw", [128, 8, 4])
        cb = sb("cb", [128, 8])
        epsc = sb("epsc", [128, 1])
        g8h = sb("g8h", [128, 8])
        negh = sb("negh", [128, 1])
        c36 = sb("c36", [36, 644], BF16)
        bif4 = sb("bif4", [4, 2])
        identf = sb("identf", [128, 128])
        bdc = sb("bdc", [8, 528])

        psb = [es.enter_context(nc.psum_tensor(f"psb{i}", [128, 512], F32)) for i in range(7)]
        psT = es.enter_context(nc.psum_tensor("psT", [128, 1024], BF16))

        P.dma('sp', identb[:], c_identb[:, :], w=['identb'])
        P.dma('sp', mask[:], c_mask[:, :], w=['mask'])
        P.dma('sp', c36[:], c_36[:, :], w=['c36'])
        P.dma('sp', identf[:], c_identf[:, :], w=['identf'])
        P.dma('sp', bdc[:], c_bd[:, :], w=['bdc'])
        with nc.allow_non_contiguous_dma(reason="small param layouts"):
            P.dma('sp', g8[:], norm_g.rearrange("(k p) -> p k", p=128), w=['g8'])
            P.dma('sp', cb[:], conv_b.rearrange("(c p) -> p c", p=128), w=['cb'])
            P.dma('sp', bif4[:], b_if.rearrange("(j h) -> h j", h=4), w=['bif4'])
            for j in range(4):
                P.dma('sp', cw[:, :, j], conv_w[j, :].rearrange("(c p) -> p c", p=128), w=['cw'])
        P.dma('sp', gf_bc[:], gf.partition_broadcast(128), w=['gf_bc'])
        P.dma('sp', gm_bc[:], gm.partition_broadcast(128), w=['gm_bc'])
        P.op('dve', lambda e: e.memset(onesf[:], 1.0), w=['onesf'])
        P.op('dve', lambda e: e.memset(epsc[:], EPS), w=['epsc'])
        P.op('dve', lambda e: e.memset(negh[:], -0.5), w=['negh'])
        P.op('dve', lambda e: e.tensor_scalar(out=g8h[:], in0=g8[:], scalar1=0.5, scalar2=None, op0=ALU.mult), r=['g8'], w=['g8h'])
        P.op('dve', lambda e: e.tensor_scalar(out=cw[:], in0=cw[:], scalar1=0.5, scalar2=None, op0=ALU.mult), r=['cw'], w=['cw'])
        P.op('dve', lambda e: e.tensor_scalar(out=cb[:], in0=cb[:], scalar1=0.5, scalar2=None, op0=ALU.mult), r=['cb'], w=['cb'])

        with ExitStack() as ies:
            wst = [sb(f"wst{i}", [128, 2048], F32, ies) for i in range(2)]
            pieces = []
            for k in range(8):
                pieces.append(('in', k, 0, 2048))
                pieces.append(('in', k, 2048, 2048))
                pieces.append(('in', k, 4096, 520))
            for pi, (kind, k, c0, wd) in enumerate(pieces):
                st = wst[pi % 2]
                key = f'wst{pi % 2}'
                src = w_in if kind == 'in' else w_out
                P.dma('sp', st[:, 0:wd], src[k * 128:(k + 1) * 128, c0:c0 + wd], w=[key])
                gk = g8[:, k:k + 1]
                gkh = g8h[:, k:k + 1]

                def cast(dst, lo, hi, eng, scaled=True, gk=gk):
                    if eng == 'act':
                        if scaled:
                            P.op('act', lambda e: e.activation(out=dst, in_=st[:, lo:hi], func=AF.Copy, scale=gk),
                                 r=[key, 'g8', 'g8h'])
                        else:
                            P.op('act', lambda e: e.activation(out=dst, in_=st[:, lo:hi], func=AF.Copy),
                                 r=[key])
                    else:
                        if scaled:
                            P.op('dve', lambda e: e.tensor_scalar(out=dst, in0=st[:, lo:hi], scalar1=gk, scalar2=None,
                                                                  op0=ALU.mult), r=[key, 'g8', 'g8h'])
                        else:
                            P.op('dve', lambda e: e.tensor_copy(out=dst, in_=st[:, lo:hi]), r=[key])
                if kind == 'in':
                    if c0 == 0:
                        cast(WT[:, k, 0:1024], 0, 1024, 'act')
                        cast(WT[:, k, 1024:1536], 1024, 1536, 'dve')
                        cast(WT[:, k, 1536:2048], 1536, 2048, 'dve', gk=gkh)
                    elif c0 == 2048:
                        cast(WF[:, k, 0:1024], 0, 1024, 'act')
                        cast(WT[:, k, 2048:2560], 1024, 1536, 'dve')
                        cast(WT[:, k, 2560:3072], 1536, 2048, 'dve', gk=gkh)
                    else:
                        cast(WT[:, k, 3072:3584], 0, 512, 'dve', gk=gkh)
                        cast(WT[:, k, 3584:3592], 512, 520, 'dve')
                else:
                    cast(WO[:, k, 0:512], 0, 512, 'act', scaled=False)
                    cast(WO[:, k, 512:1024], 512, 1024, 'dve', scaled=False)
            P.barrier()

        def run_seq(kind, si, shared_wo=None):
            prompt = kind == 'p'
            n = 128 if prompt else TS
            ntiles = SEQ // 128 if prompt else 1
            ntiles_run = min(ntiles, KTILES)
            L = ntiles * n
            xsrc = xp[si] if prompt else xs[si]
            ydst = y_p[si] if prompt else y_s[si]

            with ExitStack() as qs:
                ymixT = sb("ymixT", [128, 8, L], BF16, qs)
                if not prompt:
                    za_s = sb("za_s", [128, 512], F32, qs)
                    KgC = sb("KgC", [128, 8, 512], F32, qs)
                    Vg = sb("Vg", [128, 12, 512], F32, qs)
                    for t in range(TS):
                        for beta, sl in ((1, slice(1536 + t, 1536 + t + 509, 4)), (2, slice(t, t + 16 * 127 + 1, 16))):
                            P.dma('sp', KgC[:, t * 2 + beta - 1, :], cache_k[si, sl, :], w=[f'KgC{t * 2 + beta - 1}'])
                            P.dma('sp', Vg[:, t * 3 + beta, :], cache_v[si, sl, :], w=[f'Vg{t * 3 + beta}'])
                if prompt:
                    qT = sb("qT", [128, 4, L], BF16, qs)
                    kT = sb("kT", [128, 4, L], BF16, qs)
                    n2max = sb("n2max", [128, 16], F32, qs)
                    negB = sb("negB", [128, 8], F32, qs)

                with ExitStack() as ws:
                    xts = [sb(f"xt{i}", [128, D], F32, ws) for i in range(2)]
                    xsb = sb("xsb", [128, D], BF16, ws)
                    ss = sb("ss", [128, 4], F32, ws)
                    hnT = sb("hnT", [128, 8, 128], BF16, ws)
                    q_tok = sb("q_tok", [128, 512], F32, ws)
                    k_tok = sb("k_tok", [128, 512], F32, ws)
                    v_tok = sb("v_tok", [128, 512], F32, ws)
                    qk_bf = sb("qk_bf", [128, 2, 512], BF16, ws)
                    n2t = sb("n2t", [128, 16], F32, ws)
                    rtmp = sb("rtmp", [128, 4, 64], F32, ws)
                    cst = sb("cst", [128, 2, 64], F32, ws)
                    vst = sb("vst", [128, 8, 128], BF16, ws)
                    za_tok = sb("za_tok", [128, 512], BF16, ws)
                    vb_aug = sb("vb_aug", [128, 4, 129], BF16, ws)
                    sig_o = sb("sig_o", [128, 512], F32, ws)
                    gz = sb("gz", [128, 512], F32, ws)
                    X36 = sb("X36", [36, 6, 128], BF16, ws)
                    DG = sb("DG", [36, 4], BF16, ws)
                    XP = sb("XP", [128, 8, 131], F32, ws)
                    cacc = sb("cacc", [128, 8, 128], F32, ws)
                    sqt = cacc[:, 0:4, :].rearrange("p c x -> p (c x)")
                    qkmT = sb("qkmT", [128, 8, 128], BF16, ws)
                    Caug = sb("Caug", [128, 4, 129], F32, ws)
                    Cbf = sb("Cbf", [128, 4, 129], BF16, ws)
                    rows = sb("rows", [4, 10, 128], F32, ws)
                    rsm = sb("rsm", [4, 8], F32, ws)
                    tok4 = sb("tok4", [128, 16], F32, ws)
                    dec_bc = sb("dec_bc", [128, 4], F32, ws)
                    Dts = [sb(f"Dt{h}", [128, 128], F32, ws) for h in range(4)]
                    Dt = Dts[0]
                    sqkTs = [sb(f"sqkT{h}", [128, 128], BF16, ws) for h in range(4)]
                    numAs = [sb(f"numA{h}", [128, 129], F32, ws) for h in range(4)]
                    hms = [sb(f"hm{h}", [128, 128], F32, ws) for h in range(4)]
                    hsq = sb("hsq", [128, 128], F32, ws)
                    sm1 = sb("sm1", [128, 4, 8], F32, ws)
                    kws = [sb(f"kw{h}", [128, 128], BF16, ws) for h in range(4)]
                    yb_tok = sb("yb_tok", [128, 512], BF16, ws)

                    P.op('dve', lambda e: e.memset(vst[:], 1.0), w=['vst'])
                    P.op('dve', lambda e: e.memset(vb_aug[:], 1.0), w=['vb_aug'])
                    P.op('dve', lambda e: e.memset(X36[:], 0.0), w=['X36'])
                    P.op('dve', lambda e: e.memset(DG[:], 0.0), w=['DG'])
                    if prompt:
                        P.op('dve', lambda e: e.memset(Caug[:], 0.0), w=['Caug'])
                        P.op('dve', lambda e: e.memset(rsm[:], 0.0), w=['rsm'])
                        P.op('dve', lambda e: e.memset(XP[:], 0.0), w=['XP'])
                        P.op('dve', lambda e: e.memset(n2max[:], 0.0), w=['n2max'])
                    else:
                        P.op('dve', lambda e: e.memset(rsm[:], 0.0), w=['rsm'])
                        for h in range(4):
                            P.dma('sp', Caug[:, h, 0:128], st_C[si, h], w=['Caug'])
                        with nc.allow_non_contiguous_dma(reason="small state"):
                            P.dma('sp', Caug[:, :, 128], st_n[si].rearrange("h d -> d h"), w=['Caug'])
                            P.dma('sp', rsm[:, 0:1], st_m[si].rearrange("(h o) -> h o", o=1), w=['rsm'])
                            for j in range(3):
                                P.dma('sp', XP[:, :, j], st_conv[si, j, :].rearrange("(c p) -> p c", p=128), w=['XP'])
                    P.op('act', lambda e: e.activation(out=Cbf[:], in_=Caug[:], func=AF.Copy), r=['Caug'], w=['Cbf0', 'Cbf1', 'Cbf2', 'Cbf3'])

                    def load_x(t):
                        P.dma('sp', xts[t % 2][:n, :], xsrc[t * n:(t + 1) * n, :], w=[f'xt{t % 2}'])

                    load_x(0)
                    for t in range(ntiles_run if KSTAGE >= 1 else 0):
                        if t + 1 < ntiles:
                            load_x(t + 1)
                        xt = xts[t % 2]
                        xk = f'xt{t % 2}'
                        tok0 = t * n
                        csrc, ssrc = (c_cosp, c_sinp) if prompt else (c_coss, c_sins)
                        P.dma('sp', cst[:n, 0, :], csrc[tok0:tok0 + n, :], w=['cst'])
                        P.dma('sp', cst[:n, 1, :], ssrc[tok0:tok0 + n, :], w=['cst'])
                        def seg_norm(tt):
                            xt = xts[tt % 2]
                            xk = f'xt{tt % 2}'
                            P.op('act', lambda e: e.activation(out=xsb[:n, :], in_=xt[:n, :], func=AF.Square,
                                                               accum_out=ss[:n, 0:1]), r=[xk], w=['xsb', 'ss'])
                            P.op('dve', lambda e: e.tensor_scalar(out=ss[:n, 1:2], in0=ss[:n, 0:1], scalar1=1.0 / D, scalar2=EPS,
                                                                  op0=ALU.mult, op1=ALU.add), r=['ss'], w=['ss'])
                            P.op('pool', lambda e: e.tensor_tensor(out=ss[:n, 2:3], in0=ss[:n, 1:2], in1=negh[:n, 0:1], op=ALU.pow),
                                 r=['ss', 'negh'], w=['ss'])
                            P.op('dve', lambda e: e.tensor_scalar(out=xsb[:n, :], in0=xt[:n, :], scalar1=ss[:n, 2:3],
                                                                  scalar2=None, op0=ALU.mult), r=[xk, 'ss'], w=['xsb'])
                            for k in range(8):
                                P.op('pe', lambda e, k=k: e.transpose(psT[:, k * 128:k * 128 + n], xsb[:n, k * 128:(k + 1) * 128],
                                                                       identb[:n, :n]), r=['xsb', 'identb'], w=['psT'])
                            P.op('act', lambda e: e.activation(
                                out=hnT[:, :, :n], in_=psT[:, :].rearrange("p (k c) -> p k c", c=128)[:, :, :n], func=AF.Copy),
                                r=['psT'], w=['hnT'])

                        def seg_fproj():
                            for c in range(8):
                                bank = psb[4 + c // 4]
                                for k in range(8):
                                    P.op('pe', lambda e, c=c, k=k, bank=bank: e.matmul(
                                        bank[:, (c % 4) * 128:(c % 4) * 128 + n], lhsT=WF[:, k, c * 128:(c + 1) * 128],
                                        rhs=hnT[:, k, :n], start=(k == 0), stop=(k == 7)),
                                        r=['hnT', 'W'], w=[f'psb{4 + c // 4}'])
                            for hb in range(2):
                                eng = 'act' if hb == 0 else 'dve'
                                src = psb[4 + hb][:, :].rearrange("p (c x) -> p c x", x=128)[:, :, :n]
                                dst = XP[:, hb * 4:(hb + 1) * 4, 3:3 + n]
                                if eng == 'act':
                                    P.op('act', lambda e, src=src, dst=dst: e.activation(out=dst, in_=src, func=AF.Copy),
                                         r=[f'psb{4 + hb}'], w=['XP'])
                                else:
                                    P.op('dve', lambda e, src=src, dst=dst: e.tensor_copy(out=dst, in_=src),
                                         r=[f'psb{4 + hb}'], w=['XP'])

                        def seg_tproj():
                            def evac(c, bank, bk):
                                if c == 0:
                                    P.op('act', lambda e: e.activation(out=q_tok[:n, :], in_=bank[:n, :], func=AF.Copy),
                                         r=[bk], w=['q_tok'])
                                elif c == 1:
                                    P.op('dve', lambda e: e.tensor_copy(out=k_tok[:n, :], in_=bank[:n, :]), r=[bk], w=['k_tok'])
                                elif c == 2:
                                    P.op('act', lambda e: e.activation(out=v_tok[:n, :], in_=bank[:n, :], func=AF.Copy),
                                         r=[bk], w=['v_tok'])
                                elif c == 3:
                                    za_dst = za_tok if prompt else za_s
                                    P.op('act', lambda e: e.activation(out=za_dst[:n, :], in_=bank[:n, :], func=AF.Tanh),
                                         r=[bk], w=['za_tok'])
                                    P.op('dve', lambda e: e.scalar_tensor_tensor(out=za_dst[:n, :], in0=za_dst[:n, :], scalar=1.0,
                                                                                 in1=bank[:n, :], op0=ALU.add, op1=ALU.mult),
                                         r=[bk, 'za_tok'], w=['za_tok'])
                                elif c == 4:
                                    P.op('dve', lambda e: e.tensor_copy(
                                        out=vb_aug[:n, :, 0:128], in_=bank[:n, :].rearrange("p (h d) -> p h d", d=128)),
                                        r=[bk], w=['vb_aug'])
                                elif c == 5:
                                    P.op('act', lambda e: e.activation(out=sig_o[:n, :], in_=bank[:n, :], func=AF.Tanh),
                                         r=[bk], w=['sig_o'])
                                    P.op('pool', lambda e: e.tensor_scalar(out=sig_o[:n, :], in0=sig_o[:n, :], scalar1=0.5, scalar2=0.5,
                                                                          op0=ALU.mult, op1=ALU.add), r=['sig_o'], w=['sig_o'])
                                elif c == 6:
                                    P.op('act', lambda e: e.activation(out=gz[:n, :], in_=bank[:n, :], func=AF.Tanh),
                                         r=[bk], w=['gz'])
                                    P.op('dve', lambda e: e.scalar_tensor_tensor(out=gz[:n, :], in0=gz[:n, :], scalar=1.0,
                                                                                 in1=bank[:n, :], op0=ALU.add, op1=ALU.mult),
                                         r=[bk, 'gz'], w=['gz'])
                                    P.op('pool', lambda e: e.tensor_tensor(out=gz[:n, :], in0=gz[:n, :], in1=gm_bc[:n, :],
                                                                          op=ALU.mult), r=['gz', 'gm_bc'], w=['gz'])

                            for c in range(7):
                                bank = psb[c % 2]
                                bk = f'psb{c % 2}'
                                c0 = c * 512
                                wd = 512
                                for k in range(8):
                                    P.op('pe', lambda e, k=k, bank=bank, c0=c0, wd=wd: e.matmul(
                                        bank[:n, 0:wd], lhsT=hnT[:, k, :n], rhs=WT[:, k, c0:c0 + wd],
                                        start=(k == 0), stop=(k == 7)), r=['hnT', 'W'], w=[bk])
                                evac(c, bank, bk)

                        def seg_rope():
                            cosv = cst[:n, 0, :].rearrange("p (h j) -> p h j", j=8)
                            sinv = cst[:n, 1, :].rearrange("p (h j) -> p h j", j=8)
                            for T, tk in ((q_tok, 'q_tok'), (k_tok, 'k_tok')):
                                v3 = T[:n, :].rearrange("p (h d) -> p h d", d=64)
                                x1 = v3[:, :, 0:8]
                                x2 = v3[:, :, 8:16]
                                tt = [rtmp[:n, i, :].rearrange("p (h j) -> p h j", j=8) for i in range(4)]
                                P.op('pool', lambda e, x1=x1, tt=tt: e.tensor_tensor(out=tt[0], in0=x1, in1=cosv, op=ALU.mult),
                                     r=[tk, 'cst'], w=['rtmp'])
                                P.op('pool', lambda e, x2=x2, tt=tt: e.tensor_tensor(out=tt[1], in0=x2, in1=sinv, op=ALU.mult),
                                     r=[tk, 'cst'], w=['rtmp'])
                                P.op('pool', lambda e, x2=x2, tt=tt: e.tensor_tensor(out=tt[2], in0=x2, in1=cosv, op=ALU.mult),
                                     r=[tk, 'cst'], w=['rtmp'])
                                P.op('pool', lambda e, x1=x1, tt=tt: e.tensor_tensor(out=tt[3], in0=x1, in1=sinv, op=ALU.mult),
                                     r=[tk, 'cst'], w=['rtmp'])
                                P.op('pool', lambda e, x1=x1, tt=tt: e.tensor_tensor(out=x1, in0=tt[0], in1=tt[1], op=ALU.subtract),
                                     r=['rtmp'], w=[tk])
                                P.op('pool', lambda e, x2=x2, tt=tt: e.tensor_tensor(out=x2, in0=tt[2], in1=tt[3], op=ALU.add),
                                     r=['rtmp'], w=[tk])

                        def seg_kv():
                            if prompt:
                                P.dma('sp', p_k[si, tok0:tok0 + n, :], k_tok[:n, :], r=['k_tok'])
                                P.dma('sp', p_v[si, tok0:tok0 + n, :], v_tok[:n, :], r=['v_tok'])
                            else:
                                P.dma('sp', s_k[si, WB - TS:WB, :], k_tok[:n, :], r=['k_tok'], w=[f's_kn{si}'])
                                P.dma('sp', s_v[si, WB - TS:WB, :], v_tok[:n, :], r=['v_tok'], w=[f's_vn{si}'])
                                P.dma('sp', qs_scr[si], q_tok[:n, :], r=['q_tok'], w=['qs_scr'])
                            if prompt:
                                P.op('act', lambda e: e.activation(out=qk_bf[:n, 0, :], in_=q_tok[:n, :], func=AF.Copy),
                                     r=['q_tok'], w=['qk_bf'])
                                P.op('act', lambda e: e.activation(out=qk_bf[:n, 1, :], in_=k_tok[:n, :], func=AF.Copy),
                                     r=['k_tok'], w=['qk_bf'])
                                for w2 in range(2):
                                    for pr in range(4):
                                        P.op('pe', lambda e, w2=w2, pr=pr: e.transpose(
                                            psT[:, (w2 * 4 + pr) * 128:(w2 * 4 + pr) * 128 + n],
                                            qk_bf[:n, w2, pr * 128:(pr + 1) * 128], identb[:n, :n]),
                                            r=['qk_bf', 'identb'], w=['psT'])
                                pv = psT[:, :].rearrange("p (k c) -> p k c", c=128)
                                P.op('act', lambda e: e.activation(out=qT[:, :, tok0:tok0 + n], in_=pv[:, 0:4, :n], func=AF.Copy),
                                     r=['psT'], w=['qT'])
                                P.op('act', lambda e: e.activation(out=kT[:, :, tok0:tok0 + n], in_=pv[:, 4:8, :n], func=AF.Copy),
                                     r=['psT'], w=['kT'])
                                for w2, (T, tk) in enumerate(((q_tok, 'q_tok'), (k_tok, 'k_tok'))):
                                    P.op('pool', lambda e, T=T: e.tensor_tensor(out=sqt[:n, :], in0=T[:n, :], in1=T[:n, :],
                                                                                op=ALU.mult), r=[tk], w=['cacc'])
                                    P.op('dve', lambda e, w2=w2: e.tensor_reduce(
                                        out=n2t[:n, w2 * 8:(w2 + 1) * 8], in_=sqt[:n, :].rearrange("p (h d) -> p h d", d=64),
                                        axis=AX.X, op=ALU.add), r=['cacc'], w=['n2t'])
                                P.op('dve', lambda e: e.tensor_tensor(out=n2max[:n, :], in0=n2max[:n, :], in1=n2t[:n, :],
                                                                      op=ALU.max), r=['n2t', 'n2max'], w=['n2max'])
                                v4 = v_tok[:n, :].rearrange("p (q e d) -> p q e d", e=2, d=64)
                                vs4 = vst[:n, :, :].rearrange("p (q e) c -> p q e c", e=2)
                                P.op('pool', lambda e: e.tensor_copy(out=vs4[:, :, 0, 0:64], in_=v4[:, :, 0, :]),
                                     r=['v_tok'], w=['vst'])
                                P.op('pool', lambda e: e.tensor_copy(out=vs4[:, :, 1, 64:128], in_=v4[:, :, 1, :]),
                                     r=['v_tok'], w=['vst'])
                                P.dma('sp', vaug_scr[si, tok0:tok0 + n, :], vst[:n, :, :].rearrange("p h c -> p (h c)"),
                                      r=['vst'], w=['vaug_scr'])
                                for pr in range(4):
                                    P.op('pe', lambda e, pr=pr: e.transpose(
                                        psT[:, pr * 128:pr * 128 + n], za_tok[:n, pr * 128:(pr + 1) * 128], identb[:n, :n]),
                                        r=['za_tok', 'identb'], w=['psT'])
                                P.op('act', lambda e: e.activation(out=ymixT[:, 0:4, tok0:tok0 + n], in_=pv[:, 0:4, :n],
                                                                   func=AF.Copy), r=['psT'], w=['ymixT'])

                        def seg_conv():
                            for c in range(8):
                                P.op('act', lambda e, c=c: e.activation(
                                    out=cacc[:, c, :n], in_=XP[:, c, 0:n], func=AF.Identity, scale=cw[:, c, 0:1],
                                    bias=cb[:, c:c + 1]), r=['XP', 'cw', 'cb'], w=['cacc'])
                                for j in range(1, 4):
                                    P.op('dve', lambda e, c=c, j=j: e.scalar_tensor_tensor(
                                        out=cacc[:, c, :n], in0=XP[:, c, j:j + n], scalar=cw[:, c, j:j + 1], in1=cacc[:, c, :n],
                                        op0=ALU.mult, op1=ALU.add), r=['XP', 'cw', 'cacc'], w=['cacc'])
                            P.op('act', lambda e: e.activation(out=qkmT[:, :, :n], in_=cacc[:, :, :n], func=AF.Tanh),
                                 r=['cacc'], w=['qkmT'])
                            P.op('dve', lambda e: e.scalar_tensor_tensor(out=qkmT[:, :, :n], in0=qkmT[:, :, :n], scalar=1.0,
                                                                         in1=cacc[:, :, :n], op0=ALU.add, op1=ALU.mult),
                                 r=['cacc', 'qkmT'], w=['qkmT'])
                            if t == ntiles - 1:
                                cdst = (p_conv if prompt else s_conv)[si]
                                with nc.allow_non_contiguous_dma(reason="conv state"):
                                    for j in range(3):
                                        P.dma('sp', cdst[j, :].rearrange("(c p) -> p c", p=128), XP[:, :, n + j], r=['XP'])
                            else:
                                P.op('dve', lambda e: e.tensor_copy(out=XP[:, :, 0:3], in_=XP[:, :, n:n + 3]),
                                     r=['XP'], w=['XP'])

                        def seg_gates():
                            ps7 = psb[6]
                            for w2 in range(2):
                                for k in range(8):
                                    P.op('pe', lambda e, w2=w2, k=k: e.matmul(
                                        ps7[0:4, w2 * 128:w2 * 128 + n], lhsT=WT[:, k, 3584 + 4 * w2:3588 + 4 * w2],
                                        rhs=hnT[:, k, :n], start=(k == 0), stop=(k == 7)), r=['hnT'], w=['psb6'])
                            R_i, R_f, R_b, R_a, R_cm, R_mt, R_al, R_in, R_en, R_w = [rows[:, i, :n] for i in range(10)]
                            mprev = rsm[:, 0:1]
                            P.op('act', lambda e: e.activation(out=R_i, in_=ps7[0:4, 0:n], func=AF.Identity, bias=bif4[:, 0:1]),
                                 r=['psb6', 'bif4'], w=['rows'])
                            P.op('act', lambda e: e.activation(out=R_cm, in_=ps7[0:4, 128:128 + n], func=AF.Identity,
                                                               bias=bif4[:, 1:2]), r=['psb6', 'bif4'], w=['rows'])
                            P.op('act', lambda e: e.activation(out=R_in, in_=R_cm, func=AF.Abs), r=['rows'], w=['rows'])
                            P.op('act', lambda e: e.activation(out=R_in, in_=R_in, func=AF.Exp, scale=-1.0), r=['rows'], w=['rows'])
                            P.op('act', lambda e: e.activation(out=R_in, in_=R_in, func=AF.Ln, bias=1.0), r=['rows'], w=['rows'])
                            P.op('dve', lambda e: e.tensor_scalar(out=R_en, in0=R_cm, scalar1=0.0, scalar2=None, op0=ALU.min),
                                 r=['rows'], w=['rows'])
                            P.op('dve', lambda e: e.tensor_tensor(out=R_f, in0=R_en, in1=R_in, op=ALU.subtract),
                                 r=['rows'], w=['rows'])
                            P.op('dve', lambda e: e.tensor_tensor_scan(out=R_b, data0=onesf[0:4, :n], data1=R_f, initial=0.0,
                                                                       op0=ALU.mult, op1=ALU.add),
                                 r=['rows', 'onesf'], w=['rows'])
                            P.op('dve', lambda e: e.tensor_tensor(out=R_a, in0=R_i, in1=R_b, op=ALU.subtract),
                                 r=['rows'], w=['rows'])
                            P.op('dve', lambda e: e.tensor_tensor_scan(out=R_cm, data0=onesf[0:4, :n], data1=R_a, initial=-1e30,
                                                                       op0=ALU.mult, op1=ALU.max),
                                 r=['rows', 'onesf'], w=['rows'])
                            P.op('dve', lambda e: e.scalar_tensor_tensor(out=R_mt, in0=R_cm, scalar=mprev, in1=R_b,
                                                                         op0=ALU.max, op1=ALU.add),
                                 r=['rows', 'rsm'], w=['rows'])
                            P.op('dve', lambda e: e.tensor_tensor(out=R_al, in0=R_b, in1=R_mt, op=ALU.subtract),
                                 r=['rows'], w=['rows'])
                            P.op('act', lambda e: e.activation(out=R_in, in_=R_al, func=AF.Exp, bias=mprev),
                                 r=['rows', 'rsm'], w=['rows'])
                            P.op('act', lambda e: e.activation(out=R_en, in_=R_mt, func=AF.Exp, scale=-1.0),
                                 r=['rows'], w=['rows'])
                            P.op('dve', lambda e: e.tensor_tensor(out=rsm[:, 1:2], in0=rows[:, 2, n - 1:n], in1=rows[:, 5, n - 1:n],
                                                                  op=ALU.subtract), r=['rows'], w=['rsm'])
                            P.op('dve', lambda e: e.tensor_scalar(out=rsm[:, 3:4], in0=rsm[:, 1:2], scalar1=math.log(128.0 ** -0.5),
                                                                  scalar2=None, op0=ALU.add), r=['rsm'], w=['rsm'])
                            P.op('act', lambda e: e.activation(out=R_w, in_=R_a, func=AF.Exp, bias=rsm[:, 3:4]),
                                 r=['rows', 'rsm'], w=['rows'])
                            P.op('act', lambda e: e.activation(out=rsm[:, 2:3], in_=rsm[:, 0:1], func=AF.Exp, bias=rsm[:, 1:2]),
                                 r=['rsm'], w=['rsm'])
                            P.op('dve', lambda e: e.tensor_copy(out=rsm[:, 0:1], in_=rows[:, 5, n - 1:n]), r=['rows', 'rsm'], w=['rsm'])
                            for qi, ri in enumerate((3, 7, 8, 9, 6)):
                                P.op('dve', lambda e, qi=qi, ri=ri: e.tensor_copy(out=X36[0:4, qi, :n], in_=rows[:, ri, :n]),
                                     r=['rows'], w=['X36'])
                                P.op('dve', lambda e, qi=qi, ri=ri: e.tensor_tensor(out=X36[32:36, qi, :n], in0=rows[:, ri, :n],
                                                                                    in1=X36[0:4, qi, :n], op=ALU.subtract),
                                     r=['rows', 'X36'], w=['X36'])
                            P.op('dve', lambda e: e.tensor_copy(out=X36[0:4, 5, 0:1], in_=rsm[:, 2:3]), r=['rsm'], w=['X36'])
                            P.op('dve', lambda e: e.tensor_tensor(out=X36[32:36, 5, 0:1], in0=rsm[:, 2:3], in1=X36[0:4, 5, 0:1],
                                                                  op=ALU.subtract), r=['rsm', 'X36'], w=['X36'])
                            P.op('dve', lambda e: e.tensor_scalar(out=DG[0:4, :], in0=identb[0:4, 0:4], scalar1=X36[0:4, 5, 0:1],
                                                                  scalar2=None, op0=ALU.mult), r=['X36', 'identb'], w=['DG'])
                            P.op('dve', lambda e: e.tensor_scalar(out=DG[32:36, :], in0=identb[32:36, 32:36],
                                                                  scalar1=X36[32:36, 5, 0:1], scalar2=None, op0=ALU.mult),
                                 r=['X36', 'identb'], w=['DG'])
                            P.op('pe', lambda e: e.matmul(ps7[:, 256:260], lhsT=c36[:, 516:644], rhs=DG[:, :], start=True, stop=True),
                                 r=['c36', 'DG'], w=['psb6'])
                            P.op('act', lambda e: e.activation(out=dec_bc[:, :], in_=ps7[:, 256:260], func=AF.Copy),
                                 r=['psb6'], w=['dec_bc'])
                            for qi in range(4):
                                P.op('pe', lambda e, qi=qi: e.matmul(ps7[:n, 264 + qi * 4:268 + qi * 4], lhsT=X36[:, qi, :n],
                                                                     rhs=c36[:, 0:4], start=True, stop=True),
                                     r=['X36', 'c36'], w=['psb6'])
                            P.op('act', lambda e: e.activation(out=tok4[:n, :], in_=ps7[:n, 264:280], func=AF.Copy),
                                 r=['psb6'], w=['tok4'])

                        def seg_heads():
                            c_dk = 128.0 ** -0.5
                            HB = [(psb[h], f'psb{h}') for h in range(4)]
                            qms = [qkmT[:, h, :n] for h in range(4)]
                            kms = [qkmT[:, 4 + h, :n] for h in range(4)]
                            for h in range(4):
                                bk_, bkk = HB[h]
                                P.op('pe', lambda e, h=h, bk_=bk_: e.matmul(bk_[:n, 0:n], lhsT=kms[h], rhs=qms[h], start=True, stop=True),
                                     r=['qkmT'], w=[bkk])
                                P.op('pe', lambda e, h=h, bk_=bk_: e.matmul(bk_[:n, 128:128 + n], lhsT=c36[:, 4 + h * 128:4 + h * 128 + n],
                                                                            rhs=X36[:, 4, :n], start=True, stop=True),
                                     r=['c36', 'X36'], w=[bkk])
                            for h in range(4):
                                bk_, bkk = HB[h]
                                P.op('dve', lambda e, h=h, bk_=bk_: e.tensor_scalar(out=Dts[h][:n, :n], in0=bk_[:n, 128:128 + n],
                                                                                     scalar1=tok4[:n, h:h + 1], scalar2=0.0,
                                                                                     op0=ALU.add, op1=ALU.min),
                                     r=[bkk, 'tok4'], w=[f'Dt{h}'])
                            for h in range(4):
                                P.op('act', lambda e, h=h: e.activation(out=Dts[h][:n, :n], in_=Dts[h][:n, :n], func=AF.Exp),
                                     r=[f'Dt{h}'], w=[f'Dt{h}'])
                            for h in range(4):
                                P.op('pool', lambda e, h=h: e.tensor_tensor(out=Dts[h][:n, :n], in0=Dts[h][:n, :n],
                                                                            in1=mask[:n, 128:128 + n], op=ALU.mult),
                                     r=[f'Dt{h}', 'mask'], w=[f'Dt{h}'])
                            for h in range(4):
                                bk_, bkk = HB[h]
                                P.op('dve', lambda e, h=h, bk_=bk_: e.scalar_tensor_tensor(out=sqkTs[h][:n, :n], in0=bk_[:n, 0:n],
                                                                                            scalar=c_dk, in1=Dts[h][:n, :n],
                                                                                            op0=ALU.mult, op1=ALU.mult),
                                     r=[bkk, f'Dt{h}'], w=[f'sqkT{h}'])
                            for h in range(4):
                                bk_, bkk = HB[h]
                                P.op('pe', lambda e, h=h, bk_=bk_: e.matmul(bk_[:n, 0:129], lhsT=sqkTs[h][:n, :n], rhs=vb_aug[:n, h, :],
                                                                            start=True, stop=True), r=[f'sqkT{h}', 'vb_aug'], w=[bkk])
                                P.op('pe', lambda e, h=h, bk_=bk_: e.matmul(bk_[:n, 129:258], lhsT=qms[h], rhs=Cbf[:, h, :],
                                                                            start=True, stop=True), r=['qkmT', f'Cbf{h}'], w=[bkk])
                            for h in range(4):
                                bk_, bkk = HB[h]
                                P.op('act', lambda e, h=h, bk_=bk_: e.activation(out=numAs[h][:n, :], in_=bk_[:n, 0:129], func=AF.Copy),
                                     r=[bkk], w=[f'numA{h}'])
                            for h in range(4):
                                bk_, bkk = HB[h]
                                P.op('dve', lambda e, h=h, bk_=bk_: e.scalar_tensor_tensor(out=numAs[h][:n, :], in0=bk_[:n, 129:258],
                                                                                            scalar=tok4[:n, 4 + h:5 + h],
                                                                                            in1=numAs[h][:n, :],
                                                                                            op0=ALU.mult, op1=ALU.add),
                                     r=[bkk, 'tok4', f'numA{h}'], w=[f'numA{h}'])
                            for h in range(4):
                                P.op('act', lambda e, h=h: e.activation(out=sm1[:n, h, 5:6], in_=numAs[h][:n, 128:129], func=AF.Abs),
                                     r=[f'numA{h}'], w=[f'sm1{h}'])
                            for h in range(4):
                                P.op('dve', lambda e, h=h: e.tensor_scalar(out=sm1[:n, h, 0:1], in0=sm1[:n, h, 5:6],
                                                                           scalar1=tok4[:n, 8 + h:9 + h], scalar2=None,
                                                                           op0=ALU.max), r=[f'sm1{h}', 'tok4'], w=[f'sm1{h}'])
                                P.op('dve', lambda e, h=h: e.reciprocal(out=sm1[:n, h, 1:2], in_=sm1[:n, h, 0:1]),
                                     r=[f'sm1{h}'], w=[f'sm1{h}'])
                                P.op('dve', lambda e, h=h: e.scalar_tensor_tensor(out=hms[h][:n, :], in0=numAs[h][:n, 0:128],
                                                                                  scalar=sm1[:n, h, 1:2],
                                                                                  in1=sig_o[:n, h * 128:(h + 1) * 128],
                                                                                  op0=ALU.mult, op1=ALU.mult),
                                     r=[f'numA{h}', f'sm1{h}', 'sig_o'], w=[f'hm{h}'])
                            for h in range(4):
                                P.op('act', lambda e, h=h: e.activation(out=hsq[:n, :], in_=hms[h][:n, :], func=AF.Square,
                                                                        accum_out=sm1[:n, h, 2:3]), r=[f'hm{h}'], w=['hsq', f'sm1{h}'])
                            for h in range(4):
                                P.op('dve', lambda e, h=h: e.tensor_scalar(out=sm1[:n, h, 3:4], in0=sm1[:n, h, 2:3], scalar1=1.0 / 128,
                                                                           scalar2=EPS, op0=ALU.mult, op1=ALU.add),
                                     r=[f'sm1{h}'], w=[f'sm1{h}'])
                                P.op('pool', lambda e, h=h: e.tensor_tensor(out=sm1[:n, h, 4:5], in0=sm1[:n, h, 3:4],
                                                                            in1=negh[:n, 0:1], op=ALU.pow),
                                     r=[f'sm1{h}', 'negh'], w=[f'sm1{h}'])
                            for h in range(4):
                                P.op('dve', lambda e, h=h: e.scalar_tensor_tensor(out=yb_tok[:n, h * 128:(h + 1) * 128],
                                                                                  in0=hms[h][:n, :], scalar=sm1[:n, h, 4:5],
                                                                                  in1=gz[:n, h * 128:(h + 1) * 128],
                                                                                  op0=ALU.mult, op1=ALU.mult),
                                     r=[f'hm{h}', f'sm1{h}', 'gz'], w=['yb_tok'])
                            for h in range(4):
                                P.op('pe', lambda e, h=h: e.transpose(psT[:n, h * 128:(h + 1) * 128], kms[h], identb[:, :]),
                                     r=['qkmT', 'identb'], w=['psT'])
                            for h in range(4):
                                P.op('act', lambda e, h=h: e.activation(out=kws[h][:n, :], in_=psT[:n, h * 128:(h + 1) * 128], func=AF.Copy,
                                                                        scale=tok4[:n, 12 + h:13 + h]),
                                     r=['psT', 'tok4'], w=[f'kw{h}'])
                            for h in range(4):
                                bk_, bkk = HB[h]
                                P.op('pe', lambda e, h=h, bk_=bk_: e.matmul(bk_[:, 258:387], lhsT=kws[h][:n, :], rhs=vb_aug[:n, h, :],
                                                                            start=True, stop=True), r=[f'kw{h}', 'vb_aug'], w=[bkk])
                            for h in range(4):
                                bk_, bkk = HB[h]
                                P.op('dve', lambda e, h=h, bk_=bk_: e.scalar_tensor_tensor(out=Caug[:, h, :], in0=Caug[:, h, :],
                                                                                            scalar=dec_bc[:, h:h + 1], in1=bk_[:, 258:387],
                                                                                            op0=ALU.mult, op1=ALU.add),
                                     r=['Caug', 'dec_bc', bkk], w=['Caug'])
                            for h in range(4):
                                P.op('act', lambda e, h=h: e.activation(out=Cbf[:, h, :], in_=Caug[:, h, :], func=AF.Copy),
                                     r=['Caug'], w=[f'Cbf{h}'])

                        def seg_yb():
                            for h in range(4):
                                P.op('pe', lambda e, h=h: e.transpose(psT[:, (4 + h) * 128:(4 + h) * 128 + n],
                                                                      yb_tok[:n, h * 128:(h + 1) * 128], identb[:n, :n]),
                                     r=['yb_tok', 'identb'], w=['psT'])
                            pv = psT[:, :].rearrange("p (k c) -> p k c", c=128)
                            P.op('act', lambda e: e.activation(out=ymixT[:, 4:8, tok0:tok0 + n], in_=pv[:, 4:8, :n], func=AF.Copy),
                                 r=['psT'], w=['ymixT'])

                        if t == 0:
                            seg_norm(0)
                            seg_fproj()
                        P.merged([seg_tproj, seg_conv, seg_gates])
                        segs = [lambda: (seg_rope(), seg_kv()), seg_heads]
                        if t + 1 < ntiles_run:
                            segs.append(lambda: (seg_norm(t + 1), seg_fproj()))
                        P.merged(segs)
                        seg_yb()

                    Cd, nd, md = (p_C, p_n, p_m) if prompt else (s_C, s_n, s_m)
                    for h in range(4):
                        P.dma('sp', Cd[si, h], Caug[:, h, 0:128], r=['Caug'])
                    with nc.allow_non_contiguous_dma(reason="small state"):
                        P.dma('sp', nd[si].rearrange("h d -> d h"), Caug[:, :, 128], r=['Caug'])
                        P.dma('sp', md[si].rearrange("(h o) -> h o", o=1), rsm[:, 0:1], r=['rsm'])

                    if prompt:
                        P.dma('sp', scr_n2[si], n2max[:, :], r=['n2max'], w=['scr_n2'])
                        with nc.allow_non_contiguous_dma(reason="tiny transpose"):
                            P.dma('sp', Dt[0:16, :], scr_n2[si].rearrange("p j -> j p"), r=['scr_n2'], w=['Dt0'])
                        P.op('dve', lambda e: e.tensor_reduce(out=sm1[0:16, 0, 0:1], in_=Dt[0:16, :], axis=AX.X, op=ALU.max),
                             r=['Dt0'], w=['sm10'])
                        with nc.allow_non_contiguous_dma(reason="tiny"):
                            P.dma('sp', scr_mx[si].rearrange("(j o) -> j o", o=1), sm1[0:16, 0, 0:1], r=['sm10'], w=['scr_mx'])
                        P.dma('sp', n2t[:, :], scr_mx[si].partition_broadcast(128), r=['scr_mx'], w=['n2t'])
                        P.op('dve', lambda e: e.tensor_tensor(out=negB[:, :], in0=n2t[:, 0:8], in1=n2t[:, 8:16], op=ALU.mult),
                             r=['n2t'], w=['negB'])
                        P.op('act', lambda e: e.activation(out=negB[:, :], in_=negB[:, :], func=AF.Sqrt), r=['negB'], w=['negB'])
                        P.op('dve', lambda e: e.tensor_scalar(out=negB[:, :], in0=negB[:, :], scalar1=-0.125, scalar2=None,
                                                              op0=ALU.mult), r=['negB'], w=['negB'])
                    P.barrier()

                if shared_wo is None:
                    WO = sb("WO", [128, 8, 1024], BF16, qs)
                    wos = [sb(f"wos{i}", [128, 1024], F32, qs) for i in range(2)]
                else:
                    WO = shared_wo

                def load_wo():
                    for k in range(8):
                        P.dma('sp', wos[k % 2][:, :], w_out[k * 128:(k + 1) * 128, :], w=[f'wos{k % 2}'])
                        P.op('act', lambda e, k=k: e.activation(out=WO[:, k, 0:512], in_=wos[k % 2][:, 0:512], func=AF.Copy),
                             r=[f'wos{k % 2}'], w=['WO'])
                        P.op('dve', lambda e, k=k: e.tensor_copy(out=WO[:, k, 512:1024], in_=wos[k % 2][:, 512:1024]),
                             r=[f'wos{k % 2}'], w=['WO'])
                if prompt:
                    load_wo()

                if not prompt:
                    with ExitStack() as sa:
                        Kg = [sb(f"Kg{i}", [128, 512], F32, sa) for i in range(2)]
                        Vs = sb("Vs", [1, 4, 512], F32, sa)
                        qbc = [sb(f"qbc{i}", [128, 512], F32, sa) for i in range(2)]
                        prod = sb("prod", [128, 512], F32, sa)
                        Skh = sb("Skh", [128, 4, 32], F32, sa)
                        STs = sb("STs", [32, 388], F32, sa)
                        Pn = sb("Pn", [32, 388], F32, sa)
                        stat = sb("stat", [32, 8], F32, sa)
                        Pk = sb("Pk", [128, 4, 32], F32, sa)
                        msk = sb("msk", [8, 512], F32, sa)
                        attn = sb("attn", [4, 512], F32, sa)
                        ya = sb("ya", [4, 512], BF16, sa)
                        b = si
                        kj = 0
                        for t in range(TS):
                            qb_ = qbc[t % 2]
                            qk_ = f'qbc{t % 2}'
                            P.dma('sp', qb_[:, :], qs_scr[b, t, :].partition_broadcast(128), r=['qs_scr'], w=[qk_])
                            srcs = [
                                (s_k[b, 1916 + t:1916 + t + 128, :], s_v[b, 1916 + t:1916 + t + 128, :], 128, True),
                                (cache_k[b, slice(1536 + t, 1536 + t + 509, 4), :], cache_v[b, slice(1536 + t, 1536 + t + 509, 4), :], 128, False),
                                (cache_k[b, slice(t, t + 16 * 127 + 1, 16), :], cache_v[b, slice(t, t + 16 * 127 + 1, 16), :], 128, False),
                                (s_k[b, WB - TS + t:WB - TS + t + 1, :], s_v[b, WB - TS + t:WB - TS + t + 1, :], 1, True),
                            ]
                            for beta, (ksrc, vsrc, np_, from_out) in enumerate(srcs):
                                if beta in (1, 2):
                                    kg = KgC[:, t * 2 + beta - 1, :]
                                    kk = f'KgC{t * 2 + beta - 1}'
                                else:
                                    kg = Kg[kj % 2]
                                    kk = f'Kg{kj % 2}'
                                    kj += 1
                                    P.dma('sp', kg[0:np_, :], ksrc, r=[f's_kn{si}'] + ([f's_kc{si}'] if beta == 0 else []), w=[kk])
                                    if beta == 0:
                                        P.dma('sp', Vg[:, t * 3, :], vsrc, r=[f's_vn{si}', f's_vc{si}'], w=[f'Vg{t * 3}'])
                                    else:
                                        P.dma('sp', Vs[0:1, t, :], vsrc, r=[f's_vn{si}'], w=['Vs'])
                                P.op('dve', lambda e, kg=kg, np_=np_, qb_=qb_: e.tensor_tensor(
                                    out=prod[0:np_, :], in0=kg[0:np_, :], in1=qb_[0:np_, :], op=ALU.mult),
                                    r=[kk, qk_], w=['prod'])
                                P.op('dve', lambda e, np_=np_, beta=beta, t=t: e.tensor_reduce(
                                    out=Skh[0:np_, beta, t * 8:(t + 1) * 8],
                                    in_=prod[0:np_, :].rearrange("p (h d) -> p h d", d=64), axis=AX.X, op=ALU.add),
                                    r=['prod'], w=['Skh'])
                        psS, psP, psO, psA = psb[0], psb[1], psb[2], psb[3]
                        for beta in range(3):
                            P.op('pe', lambda e, beta=beta: e.transpose(psS[0:32, beta * 128:(beta + 1) * 128], Skh[:, beta, :],
                                                                        identf[:, :]), r=['Skh', 'identf'], w=['psb0'])
                        P.op('pe', lambda e: e.transpose(psS[0:32, 384:385], Skh[0:1, 3, :], identf[0:1, 0:1]),
                             r=['Skh', 'identf'], w=['psb0'])
                        P.op('act', lambda e: e.activation(out=STs[:, 0:385], in_=psS[0:32, 0:385], func=AF.Copy),
                             r=['psb0'], w=['STs'])
                        P.op('dve', lambda e: e.tensor_reduce(out=stat[:, 0:1], in_=STs[:, 0:385], axis=AX.X, op=ALU.max),
                             r=['STs'], w=['stat'])
                        P.op('dve', lambda e: e.tensor_scalar(out=stat[:, 1:2], in0=stat[:, 0:1], scalar1=-0.125, scalar2=None,
                                                              op0=ALU.mult), r=['stat'], w=['stat'])
                        P.op('act', lambda e: e.activation(out=Pn[:, 0:385], in_=STs[:, 0:385], func=AF.Exp, scale=0.125,
                                                           bias=stat[:, 1:2]), r=['STs', 'stat'], w=['Pn'])
                        P.op('dve', lambda e: e.tensor_scalar(out=Pn[:, 384:385], in0=Pn[:, 384:385], scalar1=3.0, scalar2=None,
                                                              op0=ALU.mult), r=['Pn'], w=['Pn'])
                        P.op('dve', lambda e: e.tensor_reduce(out=stat[:, 2:3], in_=Pn[:, 0:385], axis=AX.X, op=ALU.add),
                             r=['Pn'], w=['stat'])
                        P.op('dve', lambda e: e.reciprocal(out=stat[:, 3:4], in_=stat[:, 2:3]), r=['stat'], w=['stat'])
                        P.op('dve', lambda e: e.tensor_scalar(out=Pn[:, 0:385], in0=Pn[:, 0:385], scalar1=stat[:, 3:4],
                                                              scalar2=None, op0=ALU.mult), r=['Pn', 'stat'], w=['Pn'])
                        for beta in range(3):
                            P.op('pe', lambda e, beta=beta: e.transpose(psP[:, beta * 32:(beta + 1) * 32],
                                                                        Pn[:, beta * 128:(beta + 1) * 128], identf[0:32, 0:32]),
                                 r=['Pn', 'identf'], w=['psb1'])
                        P.op('pe', lambda e: e.transpose(psP[0:1, 96:128], Pn[:, 384:385], identf[0:32, 0:32]),
                             r=['Pn', 'identf'], w=['psb1'])
                        P.op('act', lambda e: e.activation(out=Pk[:, 0:3, :], in_=psP[:, 0:96].rearrange("p (b x) -> p b x", x=32),
                                                           func=AF.Copy), r=['psb1'], w=['Pk'])
                        P.op('act', lambda e: e.activation(out=Pk[0:1, 3, :], in_=psP[0:1, 96:128], func=AF.Copy),
                             r=['psb1'], w=['Pk'])
                        for t in range(TS):
                            for beta in range(3):
                                P.op('pe', lambda e, t=t, beta=beta: e.matmul(
                                    psO[0:8, :], lhsT=Pk[:, beta, t * 8:(t + 1) * 8], rhs=Vg[:, t * 3 + beta, :],
                                    start=(beta == 0), stop=False), r=['Pk', f'Vg{t * 3 + beta}'], w=['psb2'])
                            P.op('pe', lambda e, t=t: e.matmul(psO[0:8, :], lhsT=Pk[0:1, 3, t * 8:(t + 1) * 8], rhs=Vs[0:1, t, :],
                                                               start=False, stop=True), r=['Pk', 'Vs'], w=['psb2'])
                            P.op('dve', lambda e: e.tensor_tensor(out=msk[:, :], in0=psO[0:8, :], in1=bdc[:, 0:512], op=ALU.mult),
                                 r=['psb2', 'bdc'], w=['msk'])
                            P.op('pe', lambda e, t=t: e.matmul(psA[0:4, :], lhsT=bdc[:, 512 + 4 * t:516 + 4 * t], rhs=msk[:, :],
                                                               start=(t == 0), stop=(t == TS - 1)), r=['msk', 'bdc'], w=['psb3'])
                        P.op('act', lambda e: e.activation(out=attn[:, :], in_=psA[0:4, :], func=AF.Copy), r=['psb3'], w=['attn'])
                        P.op('dve', lambda e: e.tensor_tensor(out=ya[:, :], in0=attn[:, :], in1=za_s[0:4, :], op=ALU.mult),
                             r=['attn', 'za_tok'], w=['ya'])
                        for pr in range(4):
                            P.op('pe', lambda e, pr=pr: e.transpose(psT[:, pr * 128:pr * 128 + TS], ya[:, pr * 128:(pr + 1) * 128],
                                                                    identb[0:TS, 0:TS]), r=['ya', 'identb'], w=['psT'])
                        pvs = psT[:, :].rearrange("p (k c) -> p k c", c=128)
                        P.op('act', lambda e: e.activation(out=ymixT[:, 0:4, 0:TS], in_=pvs[:, 0:4, 0:TS], func=AF.Copy),
                             r=['psT'], w=['ymixT'])
                        P.barrier()

                if prompt and KSTAGE >= 2:
                    with ExitStack() as as_:
                        NUM = sb("NUM", [128, SEQ], F32, as_)
                        DEN = sb("DEN", [128, SEQ], F32, as_)
                        Vbufs = [sb(f"Vbuf{i}", [128, 16, 256], BF16, as_) for i in range(2)]
                        Pts = [sb(f"Pt{i}", [128, 2, 2, 128], BF16, as_) for i in range(2)]
                        mk3 = mask[:, :].rearrange("p (b x) -> p b x", x=128)
                        groups = [(pr, dil) for pr in range(4) for dil in (1, 4, 16)]

                        def load_v(gi):
                            pr, dil = groups[gi]
                            ncb = 16 // dil
                            vb_ = Vbufs[gi % 2]
                            for r_ in range(dil):
                                src = vaug_scr[si, :, pr * 256:(pr + 1) * 256].rearrange(
                                    "(c i d) x -> d i c x", d=dil, i=128)[r_]
                                P.dma('sp', vb_[:, r_ * ncb:(r_ + 1) * ncb, :], src, r=['vaug_scr'], w=[f'Vbuf{gi % 2}'])

                        def stageA(U):
                            pr, u, qsl, psl, hasp = U['pr'], U['u'], U['qsl'], U['psl'], U['hasp']
                            Pt = Pts[u]
                            b0 = 0 if hasp else 1
                            for e2 in range(2):
                                lo, hi = e2 * 64, e2 * 64 + 64
                                Sb = psb[2 * u + e2]
                                sk = f'psb{2 * u + e2}'
                                if hasp:
                                    P.op('pe', lambda e: e.matmul(Sb[:, 0:128], lhsT=kT[lo:hi, pr, psl], rhs=qT[lo:hi, pr, qsl],
                                                                  start=True, stop=True), r=['qT', 'kT'], w=[sk])
                                P.op('pe', lambda e: e.matmul(Sb[:, 128:256], lhsT=kT[lo:hi, pr, qsl], rhs=qT[lo:hi, pr, qsl],
                                                              start=True, stop=True), r=['qT', 'kT'], w=[sk])
                            for e2 in range(2):
                                hh = 2 * pr + e2
                                Sb = psb[2 * u + e2]
                                sk = f'psb{2 * u + e2}'
                                P.op('act', lambda e: e.activation(
                                    out=Pt[:, e2, b0:2, :], in_=Sb[:, b0 * 128:256].rearrange("p (b x) -> p b x", x=128),
                                    func=AF.Exp, scale=0.125, bias=negB[:, hh:hh + 1]), r=[sk, 'negB'], w=[f'Pt{u}_{e2}'])
                            for e2 in range(2):
                                eng = 'pool'
                                P.op(eng, lambda e: e.tensor_tensor(out=Pt[:, e2, b0:2, :], in0=Pt[:, e2, b0:2, :],
                                                                    in1=mk3[:, b0:2, :], op=ALU.mult),
                                     r=[f'Pt{u}_{e2}', 'mask'], w=[f'Pt{u}_{e2}'])

                        def stageB(U):
                            u, qsl, hasp, blk, gi = U['u'], U['qsl'], U['hasp'], U['blk'], U['gi']
                            Pt = Pts[u]
                            vb_ = Vbufs[gi % 2]
                            vk = f'Vbuf{gi % 2}'
                            Ob = psb[4 + u]
                            ok = f'psb{4 + u}'
                            b0 = 0 if hasp else 1
                            for (e2, r0, c0, l0) in ((0, 0, 0, 0), (1, 64, 0, 192), (0, 0, 128, 64), (1, 64, 128, 128)):
                                for bb in range(b0, 2):
                                    vbi = blk - 1 if bb == 0 else blk
                                    P.op('pe', lambda e: e.matmul(Ob[r0:r0 + 64, c0:c0 + 128],
                                                                  lhsT=vb_[:, vbi, l0:l0 + 64], rhs=Pt[:, e2, bb, :],
                                                                  start=(bb == b0), stop=(bb == 1)),
                                         r=[vk, f'Pt{u}_{e2}'], w=[ok])
                            P.op('dve', lambda e: e.tensor_tensor(out=NUM[:, qsl], in0=NUM[:, qsl], in1=Ob[:, 0:128],
                                                                  op=ALU.add), r=[ok, 'NUM'], w=['NUM'])
                            P.op('dve', lambda e: e.tensor_tensor(out=DEN[:, qsl], in0=DEN[:, qsl], in1=Ob[:, 128:256],
                                                                  op=ALU.add), r=[ok, 'DEN'], w=['DEN'])

                        ui = 0
                        load_v(0)
                        for gi, (pr, dil) in enumerate(groups):
                            if gi + 1 < len(groups):
                                load_v(gi + 1)
                            if dil == 1:
                                P.op('dve', lambda e: e.memset(NUM[:, :], 0.0), w=['NUM'])
                                P.op('pool', lambda e: e.memset(DEN[:, :], 0.0), w=['DEN'])
                            ncb = 16 // dil
                            units = []
                            for r_ in range(dil):
                                for c in range(ncb):
                                    t0 = r_ + dil * 128 * c
                                    units.append(dict(pr=pr, gi=gi, blk=r_ * ncb + c, u=ui % 2, hasp=(c > 0),
                                                      qsl=slice(t0, t0 + dil * 127 + 1, dil),
                                                      psl=slice(t0 - dil * 128, t0 - dil * 128 + dil * 127 + 1, dil)))
                                    ui += 1
                            stageA(units[0])
                            for i in range(len(units)):
                                if i + 1 < len(units):
                                    stageA(units[i + 1])
                                stageB(units[i])
                            if dil == 16:
                                P.op('dve', lambda e: e.reciprocal(out=DEN[:, :], in_=DEN[:, :]), r=['DEN'], w=['DEN'])
                                P.op('dve', lambda e: e.tensor_tensor(out=NUM[:, :], in0=NUM[:, :], in1=DEN[:, :], op=ALU.mult),
                                     r=['NUM', 'DEN'], w=['NUM'])
                                P.op('dve', lambda e: e.tensor_tensor(out=ymixT[:, pr, :], in0=NUM[:, :], in1=ymixT[:, pr, :],
                                                                      op=ALU.mult), r=['NUM', 'ymixT'], w=['ymixT'])
                        P.barrier()

                with ExitStack() as os_:
                    xts = [sb(f"xo{i}", [128, D], F32, os_) for i in range(2)]
                    yts = [sb(f"yo{i}", [128, D], F32, os_) for i in range(2)]
                    junk = sb("junk", [128, D], BF16, os_)
                    so = sb("so", [128, 4], F32, os_)
                    if not prompt and shared_wo is None:
                        load_wo()

                    def load_x2(t):
                        P.dma('sp', xts[t % 2][:n, :], xsrc[t * n:(t + 1) * n, :], w=[f'xo{t % 2}'])
                    load_x2(0)
                    for t in range(ntiles_run if KSTAGE >= 3 else 0):
                        if t + 1 < ntiles:
                            load_x2(t + 1)
                        xt = xts[t % 2]
                        xk = f'xo{t % 2}'
                        yt = yts[t % 2]
                        yk = f'yo{t % 2}'
                        tok0 = t * n
                        for c in range(2):
                            bank = psb[4 + c]
                            bk = f'psb{4 + c}'
                            for k in range(8):
                                P.op('pe', lambda e, c=c, k=k, bank=bank: e.matmul(
                                    bank[:n, :], lhsT=ymixT[:, k, tok0:tok0 + n], rhs=WO[:, k, c * 512:(c + 1) * 512],
                                    start=(k == 0), stop=(k == 7)), r=['ymixT', 'WO'], w=[bk])
                            P.op('dve', lambda e, c=c, bank=bank: e.tensor_tensor(
                                out=yt[:n, c * 512:(c + 1) * 512], in0=bank[:n, :], in1=xt[:n, c * 512:(c + 1) * 512], op=ALU.add),
                                r=[bk, xk], w=[yk])
                        P.op('act', lambda e: e.activation(out=junk[:n, :], in_=yt[:n, :], func=AF.Square, accum_out=so[:n, 0:1]),
                             r=[yk], w=['junk', 'so'])
                        P.op('dve', lambda e: e.tensor_scalar(out=so[:n, 1:2], in0=so[:n, 0:1], scalar1=1.0 / D, scalar2=EPS,
                                                              op0=ALU.mult, op1=ALU.add), r=['so'], w=['so'])
                        P.op('pool', lambda e: e.tensor_tensor(out=so[:n, 2:3], in0=so[:n, 1:2], in1=negh[:n, 0:1], op=ALU.pow),
                             r=['so', 'negh'], w=['so'])
                        P.op('dve', lambda e: e.scalar_tensor_tensor(out=yt[:n, :], in0=yt[:n, :], scalar=so[:n, 2:3],
                                                                     in1=gf_bc[:n, :], op0=ALU.mult, op1=ALU.mult),
                             r=[yk, 'so', 'gf_bc'], w=[yk])
                        P.dma('sp', ydst[tok0:tok0 + n, :], yt[:n, :], r=[yk])
                    P.barrier()

        if not KNOSAMPLE:
            with ExitStack() as sws:
                WOs = sb("WOs", [128, 8, 1024], BF16, sws)
                wst2 = [sb(f"wst2_{i}", [128, 1024], F32, sws) for i in range(1)]
                for k in range(8):
                    P.dma('sp', wst2[0][:, :], w_out[k * 128:(k + 1) * 128, :], w=['wst2_0'])
                    P.op('act', lambda e, k=k: e.activation(out=WOs[:, k, 0:512], in_=wst2[0][:, 0:512], func=AF.Copy),
                         r=['wst2_0'], w=['WO'])
                    P.op('dve', lambda e, k=k: e.tensor_copy(out=WOs[:, k, 512:1024], in_=wst2[0][:, 512:1024]),
                         r=['wst2_0'], w=['WO'])
                def shift_copy(b):
                    P.dma('sp', s_k[b, 0:WB - TS, :], cache_k[b, TS:WB, :], w=[f's_kc{b}'])
                    P.dma('sp', s_v[b, 0:WB - TS, :], cache_v[b, TS:WB, :], w=[f's_vc{b}'])
                shift_copy(0)
                for b in range(NSB):
                    if b + 1 < NSB:
                        shift_copy(b + 1)
                    if b < min(NSB, KSB):
                        run_seq('s', b, shared_wo=WOs)
                P.barrier()
        for si in range(min(NSEQ, KSEQS)):
            run_seq('p', si)
        P.barrier()
    return nc


_CACHE = {}


def _consts():
    identb = np.eye(128, dtype=np.float32).astype(ml_dtypes.bfloat16)
    identf = np.eye(128, dtype=np.float32)
    p = np.arange(128)[:, None]
    f = np.arange(128)[None, :]
    mask = np.concatenate([(p >= f), (p <= f)], axis=1).astype(np.float32).astype(ml_dtypes.bfloat16)
    sel = np.zeros((4, 4, 128), np.float32)
    for h in range(4):
        sel[h, h, :] = 1.0
    sel = sel.reshape(4, 512)
    half = 8
    inv = (np.float32(500000.0) ** (-np.arange(half, dtype=np.float32) * np.float32(2.0) / np.float32(16))).astype(np.float32)

    def tabs(pos):
        ang = pos.astype(np.float32)[:, None] * inv[None, :]
        c = np.cos(ang).astype(np.float32)
        s = np.sin(ang).astype(np.float32)
        return np.tile(c, (1, 8)), np.tile(s, (1, 8))
    cp, sp = tabs(np.arange(SEQ))
    cs, ss = tabs(PAST + np.arange(TS))
    bd = np.zeros((8, 528), np.float32)
    for h in range(8):
        bd[h, h * 64:(h + 1) * 64] = 1.0
    for t in range(4):
        bd[:, 512 + 4 * t + t] = 1.0
    c36 = np.zeros((36, 644), np.float32)
    for h in range(4):
        for base in (0, 32):
            c36[base + h, h] = 1.0
            c36[base + h, 4 + h * 128:4 + (h + 1) * 128] = 1.0
            c36[base + h, 516:644] = 1.0
    c36 = c36.astype(ml_dtypes.bfloat16)
    return dict(c_identb=identb, c_identf=identf, c_mask=mask, c_sel=sel, c_36=c36, c_bd=bd,
                c_cosp=np.ascontiguousarray(cp), c_sinp=np.ascontiguousarray(sp),
                c_coss=np.ascontiguousarray(cs), c_sins=np.ascontiguousarray(ss))


def kernel(x_prompt, x_sample, cache_win_k, cache_win_v, state_conv, state_C, state_n, state_m,
           norm_g, w_in, conv_w, conv_b, b_i, b_f, mlstm_norm_g, w_out, final_norm_g):
    f = lambda a: np.ascontiguousarray(np.asarray(a, dtype=np.float32))
    if 'nc' not in _CACHE:
        _CACHE['nc'] = build_program()
    nc = _CACHE['nc']
    consts = _consts()
    shared = dict(norm_g=f(norm_g[0]), w_in=f(w_in[0]), conv_w=f(conv_w[0]), conv_b=f(conv_b[0]),
                  b_if=f(np.concatenate([np.asarray(b_i[0]), np.asarray(b_f[0])])), gm=f(mlstm_norm_g[0]),
                  w_out=f(w_out[0]), gf=f(final_norm_g))
    shared.update(consts)
    in_maps = []
    for c in range(NCORES):
        m = dict(shared)
        m['xp'] = f(x_prompt[c * NSEQ:(c + 1) * NSEQ])
        m['xs'] = f(x_sample[c * NSB:(c + 1) * NSB])
        WBD = 8 if KNOSAMPLE else WB
        m['cache_k'] = f(cache_win_k[0, c * NSB:(c + 1) * NSB, :WBD]).reshape(NSB, WBD, 512)
        m['cache_v'] = f(cache_win_v[0, c * NSB:(c + 1) * NSB, :WBD]).reshape(NSB, WBD, 512)
        m['st_conv'] = f(state_conv[0, c * NSB:(c + 1) * NSB])
        m['st_C'] = f(state_C[0, c * NSB:(c + 1) * NSB])
        m['st_n'] = f(state_n[0, c * NSB:(c + 1) * NSB])
        m['st_m'] = f(state_m[0, c * NSB:(c + 1) * NSB])
        in_maps.append(m)
    if KONECORE:
        res = run_bass_kernel_spmd(nc, in_maps[:1], core_ids=[0])
        R = [res.results[0]] * NCORES
    else:
        res = run_bass_kernel_spmd(nc, in_maps, core_ids=list(range(NCORES)))
        R = res.results
    cat = lambda name: np.concatenate([np.asarray(r[name]) for r in R], axis=0)
    y_prompt = cat('y_p')
    y_sample = cat('y_s')
    p_k = cat('p_k').reshape(1, 16, SEQ, 8, 64)
    p_v = cat('p_v').reshape(1, 16, SEQ, 8, 64)
    p_conv = cat('p_conv')[None]
    p_C = cat('p_C')[None]
    p_n = cat('p_n')[None]
    p_m = cat('p_m')[None]
    if KNOSAMPLE:
        s_k = np.zeros((1, 32, WB, 8, 64), np.float32)
        s_v = np.zeros((1, 32, WB, 8, 64), np.float32)
    else:
        s_k = cat('s_k').reshape(1, 32, WB, 8, 64)
        s_v = cat('s_v').reshape(1, 32, WB, 8, 64)
    s_conv = cat('s_conv')[None]
    s_C = cat('s_C')[None]
    s_n = cat('s_n')[None]
    s_m = cat('s_m')[None]
    return (y_prompt, y_sample, p_k, p_v, p_conv, p_C, p_n, p_m, s_k, s_v, s_conv, s_C, s_n, s_m)
```

```python
import math
import os
KSTAGE = int(os.environ.get('KSTAGE', '3'))
KTILES = int(os.environ.get('KTILES', '999'))
KSEQS = int(os.environ.get('KSEQS', '99'))
KSUB = int(os.environ.get('KSUB', '99'))
KNOSAMPLE = int(os.environ.get('KNOSAMPLE', '0'))
KONECORE = int(os.environ.get('KONECORE', '0'))
KSB = int(os.environ.get('KSB', '99'))
from contextlib import ExitStack
import numpy as np
import ml_dtypes
import concourse.bass as bass
import concourse.mybir as mybir
from concourse.bass_utils import run_bass_kernel_spmd
from concourse.alu_op_type import AluOpType as ALU

F32 = mybir.dt.float32
BF16 = mybir.dt.bfloat16
AF = mybir.ActivationFunctionType
AX = mybir.AxisListType

NCORES = 8
D = 1024
SEQ = 2048
NSEQ = 2
NSB = 4
TS = 4
WB = 2048
EPS = 1e-6
PAST = 16384
NIN = 4616
WTC = 3592
NDS = 40


class Prog:
    def __init__(self, nc, es):
        self.nc = nc
        self.engs = {'pe': nc.tensor, 'act': nc.scalar, 'dve': nc.vector, 'pool': nc.gpsimd, 'sp': nc.sync}
        self.sem = {k: es.enter_context(nc.semaphore('sem_' + k)) for k in ['pe', 'act', 'dve', 'pool']}
        self.cnt = {k: 0 for k in self.sem}
        self.seen = {k: {} for k in self.engs}
        self.lastw = {}
        self.readers = {}
        self.dsems = [es.enter_context(nc.semaphore(f'dsem{i}')) for i in range(NDS)]
        self.dcnt = [0] * NDS
        self.dnext = 0
        self.rec = None

    def _wait(self, e, dep):
        src, ticket = dep
        if src == e and e == 'pe':
            return
        if self.seen[e].get(src, 0) >= ticket:
            return
        sem = self.sem[src] if isinstance(src, str) else self.dsems[src]
        self.engs[e].wait_ge(sem, ticket)
        self.seen[e][src] = ticket

    def _deps(self, e, r, w):
        deps = []
        for x in r:
            if x in self.lastw:
                deps.append(self.lastw[x])
        for x in w:
            if x in self.lastw:
                deps.append(self.lastw[x])
            for s, tk in self.readers.get(x, {}).items():
                deps.append((s, tk))
        for d in deps:
            self._wait(e, d)

    def _reg(self, tk, r, w):
        for x in r:
            self.readers.setdefault(x, {})[tk[0]] = tk[1]
        for x in w:
            self.lastw[x] = tk
            self.readers[x] = {}

    @staticmethod
    def _excl(r, w):
        w2 = list(w) + [x for x in r if x.startswith('ps')]
        r2 = [x for x in r if not x.startswith('ps')]
        return r2, w2

    def merged(self, segs):
        lists = []
        for seg in segs:
            self.rec = []
            seg()
            lists.append(self.rec)
        self.rec = None
        idx = [0] * len(lists)
        while True:
            best, bf = None, None
            for i, L in enumerate(lists):
                if idx[i] < len(L):
                    f = (idx[i] + 1) / len(L)
                    if bf is None or f < bf:
                        best, bf = i, f
            if best is None:
                break
            kind, args, kw = lists[best][idx[best]]
            idx[best] += 1
            (self.op if kind == 'op' else self.dma)(*args, **kw)

    def op(self, e, fn, r=(), w=()):
        if getattr(self, 'rec', None) is not None:
            self.rec.append(('op', (e, fn, list(r), list(w)), {}))
            return
        r, w = self._excl(r, w)
        self._deps(e, r, w)
        ins = fn(self.engs[e])
        self.cnt[e] += 1
        ins.then_inc(self.sem[e], 1)
        self._reg((e, self.cnt[e]), r, w)

    def dma(self, q, out, in_, r=(), w=(), **kw):
        if getattr(self, 'rec', None) is not None:
            self.rec.append(('dma', (q, out, in_, list(r), list(w)), kw))
            return
        self._deps(q, r, w)
        i = self.dnext
        self.dnext = (i + 1) % NDS
        if self.dcnt[i] > 0:
            self._wait(q, (i, self.dcnt[i]))
        with self.nc.allow_non_contiguous_dma(reason="strided/small layouts"):
            self.engs[q].dma_start(out=out, in_=in_, **kw).then_inc(self.dsems[i], 16)
        self.dcnt[i] += 16
        self._reg((i, self.dcnt[i]), r, w)

    def barrier(self):
        for e in self.engs:
            for s in self.sem:
                if self.cnt[s] > 0:
                    self._wait(e, (s, self.cnt[s]))
            for i in range(NDS):
                if self.dcnt[i] > 0:
                    self._wait(e, (i, self.dcnt[i]))


def build_program():
    nc = bass.Bass("TRN2", target_bir_lowering=False)

    def din(name, shape, dt=F32):
        return nc.dram_tensor(name, list(shape), dt, kind="ExternalInput").ap()

    def dout(name, shape):
        return nc.dram_tensor(name, list(shape), F32, kind="ExternalOutput").ap()

    def dscr(name, shape, dt):
        return nc.dram_tensor(name, list(shape), dt, kind="Internal").ap()

    xp = din("xp", [NSEQ, SEQ, D])
    xs = din("xs", [NSB, TS, D])
    WBD = 8 if KNOSAMPLE else WB
    cache_k = din("cache_k", [NSB, WBD, 512])
    cache_v = din("cache_v", [NSB, WBD, 512])
    st_conv = din("st_conv", [NSB, 3, D])
    st_C = din("st_C", [NSB, 4, 128, 128])
    st_n = din("st_n", [NSB, 4, 128])
    st_m = din("st_m", [NSB, 4])
    norm_g = din("norm_g", [D])
    w_in = din("w_in", [D, NIN])
    conv_w = din("conv_w", [4, D])
    conv_b = din("conv_b", [D])
    b_if = din("b_if", [8])
    gm = din("gm", [512])
    w_out = din("w_out", [D, D])
    gf = din("gf", [D])
    c_identb = din("c_identb", [128, 128], BF16)
    c_identf = din("c_identf", [128, 128])
    c_mask = din("c_mask", [128, 256], BF16)
    c_sel = din("c_sel", [4, 512])
    c_36 = din("c_36", [36, 644], BF16)
    c_bd = din("c_bd", [8, 528])
    c_cosp = din("c_cosp", [SEQ, 64])
    c_sinp = din("c_sinp", [SEQ, 64])
    c_coss = din("c_coss", [TS, 64])
    c_sins = din("c_sins", [TS, 64])

    y_p = dout("y_p", [NSEQ, SEQ, D])
    y_s = dout("y_s", [NSB, TS, D])
    p_k = dout("p_k", [NSEQ, SEQ, 512])
    p_v = dout("p_v", [NSEQ, SEQ, 512])
    p_conv = dout("p_conv", [NSEQ, 3, D])
    p_C = dout("p_C", [NSEQ, 4, 128, 128])
    p_n = dout("p_n", [NSEQ, 4, 128])
    p_m = dout("p_m", [NSEQ, 4])
    s_k = dout("s_k", [NSB, WBD, 512])
    s_v = dout("s_v", [NSB, WBD, 512])
    s_conv = dout("s_conv", [NSB, 3, D])
    s_C = dout("s_C", [NSB, 4, 128, 128])
    s_n = dout("s_n", [NSB, 4, 128])
    s_m = dout("s_m", [NSB, 4])

    vaug_scr = dscr("vaug_scr", [NSEQ, SEQ, 1024], BF16)
    scr_n2 = dscr("scr_n2", [NSEQ, 128, 16], F32)
    scr_mx = dscr("scr_mx", [NSEQ, 16], F32)
    qs_scr = dscr("qs_scr", [NSB, TS, 512], F32)

    es = ExitStack()
    with es:
        P = Prog(nc, es)

        uid = [0]

        def sb(name, shape, dt=F32, stack=es):
            uid[0] += 1
            return stack.enter_context(nc.sbuf_tensor(f"{name}_{uid[0]}", list(shape), dt))

        WT = sb("WT", [128, 8, WTC], BF16)
        WF = sb("WF", [128, 8, 1024], BF16)
        identb = sb("identb", [128, 128], BF16)
        mask = sb("mask", [128, 256], BF16)
        onesf = sb("onesf", [128, 128])
        g8 = sb("g8", [128, 8])
        gf_bc = sb("gf_bc", [128, D])
        gm_bc = sb("gm_bc", [128, 512])
        cw = sb("cw", [128, 8, 4])
        cb = sb("cb", [128, 8])
        epsc = sb("epsc", [128, 1])
        g8h = sb("g8h", [128, 8])
        negh = sb("negh", [128, 1])
        c36 = sb("c36", [36, 644], BF16)
        bif4 = sb("bif4", [4, 2])
        identf = sb("identf", [128, 128])
        bdc = sb("bdc", [8, 528])

        psb = [es.enter_context(nc.psum_tensor(f"psb{i}", [128, 512], F32)) for i in range(7)]
        psT = es.enter_context(nc.psum_tensor("psT", [128, 1024], BF16))

        P.dma('sp', identb[:], c_identb[:, :], w=['identb'])
        P.dma('sp', mask[:], c_mask[:, :], w=['mask'])
        P.dma('sp', c36[:], c_36[:, :], w=['c36'])
        P.dma('sp', identf[:], c_identf[:, :], w=['identf'])
        P.dma('sp', bdc[:], c_bd[:, :], w=['bdc'])
        with nc.allow_non_contiguous_dma(reason="small param layouts"):
            P.dma('sp', g8[:], norm_g.rearrange("(k p) -> p k", p=128), w=['g8'])
            P.dma('sp', cb[:], conv_b.rearrange("(c p) -> p c", p=128), w=['cb'])
            P.dma('sp', bif4[:], b_if.rearrange("(j h) -> h j", h=4), w=['bif4'])
            for j in range(4):
                P.dma('sp', cw[:, :, j], conv_w[j, :].rearrange("(c p) -> p c", p=128), w=['cw'])
        P.dma('sp', gf_bc[:], gf.partition_broadcast(128), w=['gf_bc'])
        P.dma('sp', gm_bc[:], gm.partition_broadcast(128), w=['gm_bc'])
        P.op('dve', lambda e: e.memset(onesf[:], 1.0), w=['onesf'])
        P.op('dve', lambda e: e.memset(epsc[:], EPS), w=['epsc'])
        P.op('dve', lambda e: e.memset(negh[:], -0.5), w=['negh'])
        P.op('dve', lambda e: e.tensor_scalar(out=g8h[:], in0=g8[:], scalar1=0.5, scalar2=None, op0=ALU.mult), r=['g8'], w=['g8h'])
        P.op('dve', lambda e: e.tensor_scalar(out=cw[:], in0=cw[:], scalar1=0.5, scalar2=None, op0=ALU.mult), r=['cw'], w=['cw'])
        P.op('dve', lambda e: e.tensor_scalar(out=cb[:], in0=cb[:], scalar1=0.5, scalar2=None, op0=ALU.mult), r=['cb'], w=['cb'])

        with ExitStack() as ies:
            wst = [sb(f"wst{i}", [128, 2048], F32, ies) for i in range(4)]
            pieces = []
            for k in range(8):
                pieces.append(('in', k, 0, 2048))
                pieces.append(('in', k, 2048, 2048))
                pieces.append(('in', k, 4096, 520))
            for pi, (kind, k, c0, wd) in enumerate(pieces):
                st = wst[pi % 4]
                key = f'wst{pi % 4}'
                src = w_in if kind == 'in' else w_out
                P.dma('sp', st[:, 0:wd], src[k * 128:(k + 1) * 128, c0:c0 + wd], w=[key])
                gk = g8[:, k:k + 1]
                gkh = g8h[:, k:k + 1]

                def cast(dst, lo, hi, eng, scaled=True, gk=gk):
                    if eng == 'act':
                        if scaled:
                            P.op('act', lambda e: e.activation(out=dst, in_=st[:, lo:hi], func=AF.Copy, scale=gk),
                                 r=[key, 'g8', 'g8h'])
                        else:
                            P.op('act', lambda e: e.activation(out=dst, in_=st[:, lo:hi], func=AF.Copy),
                                 r=[key])
                    else:
                        if scaled:
                            P.op('dve', lambda e: e.tensor_scalar(out=dst, in0=st[:, lo:hi], scalar1=gk, scalar2=None,
                                                                  op0=ALU.mult), r=[key, 'g8', 'g8h'])
                        else:
                            P.op('dve', lambda e: e.tensor_copy(out=dst, in_=st[:, lo:hi]), r=[key])
                if kind == 'in':
                    if c0 == 0:
                        cast(WT[:, k, 0:1024], 0, 1024, 'act')
                        cast(WT[:, k, 1024:1536], 1024, 1536, 'dve')
                        cast(WT[:, k, 1536:2048], 1536, 2048, 'dve', gk=gkh)
                    elif c0 == 2048:
                        cast(WF[:, k, 0:1024], 0, 1024, 'act')
                        cast(WT[:, k, 2048:2560], 1024, 1536, 'dve')
                        cast(WT[:, k, 2560:3072], 1536, 2048, 'dve', gk=gkh)
                    else:
                        cast(WT[:, k, 3072:3584], 0, 512, 'dve', gk=gkh)
                        cast(WT[:, k, 3584:3592], 512, 520, 'dve')
                else:
                    cast(WO[:, k, 0:512], 0, 512, 'act', scaled=False)
                    cast(WO[:, k, 512:1024], 512, 1024, 'dve', scaled=False)
            P.barrier()

        def run_seq(kind, si, shared_wo=None):
            prompt = kind == 'p'
            n = 128 if prompt else TS
            ntiles = SEQ // 128 if prompt else 1
            ntiles_run = min(ntiles, KTILES)
            L = ntiles * n
            xsrc = xp[si] if prompt else xs[si]
            ydst = y_p[si] if prompt else y_s[si]

            with ExitStack() as qs:
                ymixT = sb("ymixT", [128, 8, L], BF16, qs)
                if not prompt:
                    za_s = sb("za_s", [128, 512], F32, qs)
                    KgC = sb("KgC", [128, 8, 512], F32, qs)
                    Vg = sb("Vg", [128, 12, 512], F32, qs)
                    for t in range(TS):
                        for beta, sl in ((1, slice(1536 + t, 1536 + t + 509, 4)), (2, slice(t, t + 16 * 127 + 1, 16))):
                            P.dma('sp', KgC[:, t * 2 + beta - 1, :], cache_k[si, sl, :], w=[f'KgC{t * 2 + beta - 1}'])
                            P.dma('sp', Vg[:, t * 3 + beta, :], cache_v[si, sl, :], w=[f'Vg{t * 3 + beta}'])
                if prompt:
                    qT = sb("qT", [128, 4, L], BF16, qs)
                    kT = sb("kT", [128, 4, L], BF16, qs)
                    n2max = sb("n2max", [128, 16], F32, qs)
                    negB = sb("negB", [128, 8], F32, qs)

                with ExitStack() as ws:
                    xts = [sb(f"xt{i}", [128, D], F32, ws) for i in range(2)]
                    xsb = sb("xsb", [128, D], BF16, ws)
                    ss = sb("ss", [128, 4], F32, ws)
                    hnT = sb("hnT", [128, 8, 128], BF16, ws)
                    q_tok = sb("q_tok", [128, 512], F32, ws)
                    k_tok = sb("k_tok", [128, 512], F32, ws)
                    v_tok = sb("v_tok", [128, 512], F32, ws)
                    qk_bf = sb("qk_bf", [128, 2, 512], BF16, ws)
                    n2t = sb("n2t", [128, 16], F32, ws)
                    rtmp = sb("rtmp", [128, 4, 64], F32, ws)
                    cst = sb("cst", [128, 2, 64], F32, ws)
                    vst = sb("vst", [128, 8, 128], BF16, ws)
                    za_tok = sb("za_tok", [128, 512], BF16, ws)
                    vb_aug = sb("vb_aug", [128, 4, 129], BF16, ws)
                    sig_o = sb("sig_o", [128, 512], F32, ws)
                    gz = sb("gz", [128, 512], F32, ws)
                    X36 = sb("X36", [36, 6, 128], BF16, ws)
                    DG = sb("DG", [36, 4], BF16, ws)
                    XP = sb("XP", [128, 8, 131], F32, ws)
                    cacc = sb("cacc", [128, 8, 128], F32, ws)
                    sqt = cacc[:, 0:4, :].rearrange("p c x -> p (c x)")
                    qkmT = sb("qkmT", [128, 8, 128], BF16, ws)
                    Caug = sb("Caug", [128, 4, 129], F32, ws)
                    Cbf = sb("Cbf", [128, 4, 129], BF16, ws)
                    rows = sb("rows", [4, 10, 128], F32, ws)
                    rsm = sb("rsm", [4, 8], F32, ws)
                    tok4 = sb("tok4", [128, 16], F32, ws)
                    dec_bc = sb("dec_bc", [128, 4], F32, ws)
                    Dts = [sb(f"Dt{h}", [128, 128], F32, ws) for h in range(4)]
                    Dt = Dts[0]
                    sqkTs = [sb(f"sqkT{h}", [128, 128], BF16, ws) for h in range(4)]
                    numAs = [sb(f"numA{h}", [128, 129], F32, ws) for h in range(4)]
                    hms = [sb(f"hm{h}", [128, 128], F32, ws) for h in range(4)]
                    hsq = sb("hsq", [128, 128], F32, ws)
                    sm1 = sb("sm1", [128, 4, 8], F32, ws)
                    kws = [sb(f"kw{h}", [128, 128], BF16, ws) for h in range(4)]
                    yb_tok = sb("yb_tok", [128, 512], BF16, ws)

                    P.op('dve', lambda e: e.memset(vst[:], 1.0), w=['vst'])
                    P.op('dve', lambda e: e.memset(vb_aug[:], 1.0), w=['vb_aug'])
                    P.op('dve', lambda e: e.memset(X36[:], 0.0), w=['X36'])
                    P.op('dve', lambda e: e.memset(DG[:], 0.0), w=['DG'])
                    if prompt:
                        P.op('dve', lambda e: e.memset(Caug[:], 0.0), w=['Caug'])
                        P.op('dve', lambda e: e.memset(rsm[:], 0.0), w=['rsm'])
                        P.op('dve', lambda e: e.memset(XP[:], 0.0), w=['XP'])
                        P.op('dve', lambda e: e.memset(n2max[:], 0.0), w=['n2max'])
                    else:
                        P.op('dve', lambda e: e.memset(rsm[:], 0.0), w=['rsm'])
                        for h in range(4):
                            P.dma('sp', Caug[:, h, 0:128], st_C[si, h], w=['Caug'])
                        with nc.allow_non_contiguous_dma(reason="small state"):
                            P.dma('sp', Caug[:, :, 128], st_n[si].rearrange("h d -> d h"), w=['Caug'])
                            P.dma('sp', rsm[:, 0:1], st_m[si].rearrange("(h o) -> h o", o=1), w=['rsm'])
                            for j in range(3):
                                P.dma('sp', XP[:, :, j], st_conv[si, j, :].rearrange("(c p) -> p c", p=128), w=['XP'])
                    P.op('act', lambda e: e.activation(out=Cbf[:], in_=Caug[:], func=AF.Copy), r=['Caug'], w=['Cbf0', 'Cbf1', 'Cbf2', 'Cbf3'])

                    def load_x(t):
                        P.dma('sp', xts[t % 2][:n, :], xsrc[t * n:(t + 1) * n, :], w=[f'xt{t % 2}'])

                    load_x(0)
                    for t in range(ntiles_run if KSTAGE >= 1 else 0):
                        if t + 1 < ntiles:
                            load_x(t + 1)
                        xt = xts[t % 2]
                        xk = f'xt{t % 2}'
                        tok0 = t * n
                        csrc, ssrc = (c_cosp, c_sinp) if prompt else (c_coss, c_sins)
                        P.dma('sp', cst[:n, 0, :], csrc[tok0:tok0 + n, :], w=['cst'])
                        P.dma('sp', cst[:n, 1, :], ssrc[tok0:tok0 + n, :], w=['cst'])
                        def seg_norm(tt):
                            xt = xts[tt % 2]
                            xk = f'xt{tt % 2}'
                            P.op('act', lambda e: e.activation(out=xsb[:n, :], in_=xt[:n, :], func=AF.Square,
                                                               accum_out=ss[:n, 0:1]), r=[xk], w=['xsb', 'ss'])
                            P.op('dve', lambda e: e.tensor_scalar(out=ss[:n, 1:2], in0=ss[:n, 0:1], scalar1=1.0 / D, scalar2=EPS,
                                                                  op0=ALU.mult, op1=ALU.add), r=['ss'], w=['ss'])
                            P.op('pool', lambda e: e.tensor_tensor(out=ss[:n, 2:3], in0=ss[:n, 1:2], in1=negh[:n, 0:1], op=ALU.pow),
                                 r=['ss', 'negh'], w=['ss'])
                            P.op('dve', lambda e: e.tensor_scalar(out=xsb[:n, :], in0=xt[:n, :], scalar1=ss[:n, 2:3],
                                                                  scalar2=None, op0=ALU.mult), r=[xk, 'ss'], w=['xsb'])
                            for k in range(8):
                                P.op('pe', lambda e, k=k: e.transpose(psT[:, k * 128:k * 128 + n], xsb[:n, k * 128:(k + 1) * 128],
                                                                       identb[:n, :n]), r=['xsb', 'identb'], w=['psT'])
                            P.op('act', lambda e: e.activation(
                                out=hnT[:, :, :n], in_=psT[:, :].rearrange("p (k c) -> p k c", c=128)[:, :, :n], func=AF.Copy),
                                r=['psT'], w=['hnT'])

                        def seg_fproj():
                            for c in range(8):
                                bank = psb[4 + c // 4]
                                for k in range(8):
                                    P.op('pe', lambda e, c=c, k=k, bank=bank: e.matmul(
                                        bank[:, (c % 4) * 128:(c % 4) * 128 + n], lhsT=WF[:, k, c * 128:(c + 1) * 128],
                                        rhs=hnT[:, k, :n], start=(k == 0), stop=(k == 7)),
                                        r=['hnT', 'W'], w=[f'psb{4 + c // 4}'])
                            for hb in range(2):
                                eng = 'act' if hb == 0 else 'dve'
                                src = psb[4 + hb][:, :].rearrange("p (c x) -> p c x", x=128)[:, :, :n]
                                dst = XP[:, hb * 4:(hb + 1) * 4, 3:3 + n]
                                if eng == 'act':
                                    P.op('act', lambda e, src=src, dst=dst: e.activation(out=dst, in_=src, func=AF.Copy),
                                         r=[f'psb{4 + hb}'], w=['XP'])
                                else:
                                    P.op('dve', lambda e, src=src, dst=dst: e.tensor_copy(out=dst, in_=src),
                                         r=[f'psb{4 + hb}'], w=['XP'])

                        def seg_tproj():
                            def evac(c, bank, bk):
                                if c == 0:
                                    P.op('act', lambda e: e.activation(out=q_tok[:n, :], in_=bank[:n, :], func=AF.Copy),
                                         r=[bk], w=['q_tok'])
                                elif c == 1:
                                    P.op('dve', lambda e: e.tensor_copy(out=k_tok[:n, :], in_=bank[:n, :]), r=[bk], w=['k_tok'])
                                elif c == 2:
                                    P.op('act', lambda e: e.activation(out=v_tok[:n, :], in_=bank[:n, :], func=AF.Copy),
                                         r=[bk], w=['v_tok'])
                                elif c == 3:
                                    za_dst = za_tok if prompt else za_s
                                    P.op('act', lambda e: e.activation(out=za_dst[:n, :], in_=bank[:n, :], func=AF.Tanh),
                                         r=[bk], w=['za_tok'])
                                    P.op('dve', lambda e: e.scalar_tensor_tensor(out=za_dst[:n, :], in0=za_dst[:n, :], scalar=1.0,
                                                                                 in1=bank[:n, :], op0=ALU.add, op1=ALU.mult),
                                         r=[bk, 'za_tok'], w=['za_tok'])
                                elif c == 4:
                                    P.op('dve', lambda e: e.tensor_copy(
                                        out=vb_aug[:n, :, 0:128], in_=bank[:n, :].rearrange("p (h d) -> p h d", d=128)),
                                        r=[bk], w=['vb_aug'])
                                elif c == 5:
                                    P.op('act', lambda e: e.activation(out=sig_o[:n, :], in_=bank[:n, :], func=AF.Tanh),
                                         r=[bk], w=['sig_o'])
                                    P.op('dve', lambda e: e.tensor_scalar(out=sig_o[:n, :], in0=sig_o[:n, :], scalar1=0.5, scalar2=0.5,
                                                                          op0=ALU.mult, op1=ALU.add), r=['sig_o'], w=['sig_o'])
                                elif c == 6:
                                    P.op('act', lambda e: e.activation(out=gz[:n, :], in_=bank[:n, :], func=AF.Tanh),
                                         r=[bk], w=['gz'])
                                    P.op('dve', lambda e: e.scalar_tensor_tensor(out=gz[:n, :], in0=gz[:n, :], scalar=1.0,
                                                                                 in1=bank[:n, :], op0=ALU.add, op1=ALU.mult),
                                         r=[bk, 'gz'], w=['gz'])
                                    P.op('dve', lambda e: e.tensor_tensor(out=gz[:n, :], in0=gz[:n, :], in1=gm_bc[:n, :],
                                                                          op=ALU.mult), r=['gz', 'gm_bc'], w=['gz'])

                            for c in range(7):
                                bank = psb[c % 2]
                                bk = f'psb{c % 2}'
                                c0 = c * 512
                                wd = 512
                                for k in range(8):
                                    P.op('pe', lambda e, k=k, bank=bank, c0=c0, wd=wd: e.matmul(
                                        bank[:n, 0:wd], lhsT=hnT[:, k, :n], rhs=WT[:, k, c0:c0 + wd],
                                        start=(k == 0), stop=(k == 7)), r=['hnT', 'W'], w=[bk])
                                evac(c, bank, bk)

                        def seg_rope():
                            cosv = cst[:n, 0, :].rearrange("p (h j) -> p h j", j=8)
                            sinv = cst[:n, 1, :].rearrange("p (h j) -> p h j", j=8)
                            for T, tk in ((q_tok, 'q_tok'), (k_tok, 'k_tok')):
                                v3 = T[:n, :].rearrange("p (h d) -> p h d", d=64)
                                x1 = v3[:, :, 0:8]
                                x2 = v3[:, :, 8:16]
                                tt = [rtmp[:n, i, :].rearrange("p (h j) -> p h j", j=8) for i in range(4)]
                                P.op('dve', lambda e, x1=x1, tt=tt: e.tensor_tensor(out=tt[0], in0=x1, in1=cosv, op=ALU.mult),
                                     r=[tk, 'cst'], w=['rtmp'])
                                P.op('dve', lambda e, x2=x2, tt=tt: e.tensor_tensor(out=tt[1], in0=x2, in1=sinv, op=ALU.mult),
                                     r=[tk, 'cst'], w=['rtmp'])
                                P.op('dve', lambda e, x2=x2, tt=tt: e.tensor_tensor(out=tt[2], in0=x2, in1=cosv, op=ALU.mult),
                                     r=[tk, 'cst'], w=['rtmp'])
                                P.op('dve', lambda e, x1=x1, tt=tt: e.tensor_tensor(out=tt[3], in0=x1, in1=sinv, op=ALU.mult),
                                     r=[tk, 'cst'], w=['rtmp'])
                                P.op('dve', lambda e, x1=x1, tt=tt: e.tensor_tensor(out=x1, in0=tt[0], in1=tt[1], op=ALU.subtract),
                                     r=['rtmp'], w=[tk])
                                P.op('dve', lambda e, x2=x2, tt=tt: e.tensor_tensor(out=x2, in0=tt[2], in1=tt[3], op=ALU.add),
                                     r=['rtmp'], w=[tk])

                        def seg_kv():
                            if prompt:
                                P.dma('sp', p_k[si, tok0:tok0 + n, :], k_tok[:n, :], r=['k_tok'])
                                P.dma('sp', p_v[si, tok0:tok0 + n, :], v_tok[:n, :], r=['v_tok'])
                            else:
                                P.dma('sp', s_k[si, WB - TS:WB, :], k_tok[:n, :], r=['k_tok'], w=[f's_kn{si}'])
                                P.dma('sp', s_v[si, WB - TS:WB, :], v_tok[:n, :], r=['v_tok'], w=[f's_vn{si}'])
                                P.dma('sp', qs_scr[si], q_tok[:n, :], r=['q_tok'], w=['qs_scr'])
                            if prompt:
                                P.op('act', lambda e: e.activation(out=qk_bf[:n, 0, :], in_=q_tok[:n, :], func=AF.Copy),
                                     r=['q_tok'], w=['qk_bf'])
                                P.op('act', lambda e: e.activation(out=qk_bf[:n, 1, :], in_=k_tok[:n, :], func=AF.Copy),
                                     r=['k_tok'], w=['qk_bf'])
                                for w2 in range(2):
                                    for pr in range(4):
                                        P.op('pe', lambda e, w2=w2, pr=pr: e.transpose(
                                            psT[:, (w2 * 4 + pr) * 128:(w2 * 4 + pr) * 128 + n],
                                            qk_bf[:n, w2, pr * 128:(pr + 1) * 128], identb[:n, :n]),
                                            r=['qk_bf', 'identb'], w=['psT'])
                                pv = psT[:, :].rearrange("p (k c) -> p k c", c=128)
                                P.op('act', lambda e: e.activation(out=qT[:, :, tok0:tok0 + n], in_=pv[:, 0:4, :n], func=AF.Copy),
                                     r=['psT'], w=['qT'])
                                P.op('act', lambda e: e.activation(out=kT[:, :, tok0:tok0 + n], in_=pv[:, 4:8, :n], func=AF.Copy),
                                     r=['psT'], w=['kT'])
                                for w2, (T, tk) in enumerate(((q_tok, 'q_tok'), (k_tok, 'k_tok'))):
                                    P.op('dve', lambda e, T=T: e.tensor_tensor(out=sqt[:n, :], in0=T[:n, :], in1=T[:n, :],
                                                                                op=ALU.mult), r=[tk], w=['cacc'])
                                    P.op('dve', lambda e, w2=w2: e.tensor_reduce(
                                        out=n2t[:n, w2 * 8:(w2 + 1) * 8], in_=sqt[:n, :].rearrange("p (h d) -> p h d", d=64),
                                        axis=AX.X, op=ALU.add), r=['cacc'], w=['n2t'])
                                P.op('dve', lambda e: e.tensor_tensor(out=n2max[:n, :], in0=n2max[:n, :], in1=n2t[:n, :],
                                                                      op=ALU.max), r=['n2t', 'n2max'], w=['n2max'])
                                v4 = v_tok[:n, :].rearrange("p (q e d) -> p q e d", e=2, d=64)
                                vs4 = vst[:n, :, :].rearrange("p (q e) c -> p q e c", e=2)
                                P.op('dve', lambda e: e.tensor_copy(out=vs4[:, :, 0, 0:64], in_=v4[:, :, 0, :]),
                                     r=['v_tok'], w=['vst'])
                                P.op('dve', lambda e: e.tensor_copy(out=vs4[:, :, 1, 64:128], in_=v4[:, :, 1, :]),
                                     r=['v_tok'], w=['vst'])
                                P.dma('sp', vaug_scr[si, tok0:tok0 + n, :], vst[:n, :, :].rearrange("p h c -> p (h c)"),
                                      r=['vst'], w=['vaug_scr'])
                                for pr in range(4):
                                    P.op('pe', lambda e, pr=pr: e.transpose(
                                        psT[:, pr * 128:pr * 128 + n], za_tok[:n, pr * 128:(pr + 1) * 128], identb[:n, :n]),
                                        r=['za_tok', 'identb'], w=['psT'])
                                P.op('act', lambda e: e.activation(out=ymixT[:, 0:4, tok0:tok0 + n], in_=pv[:, 0:4, :n],
                                                                   func=AF.Copy), r=['psT'], w=['ymixT'])

                        def seg_conv():
                            for c in range(8):
                                P.op('dve', lambda e, c=c: e.tensor_scalar(
                                    out=cacc[:, c, :n], in0=XP[:, c, 0:n], scalar1=cw[:, c, 0:1], scalar2=cb[:, c:c + 1],
                                    op0=ALU.mult, op1=ALU.add), r=['XP', 'cw', 'cb'], w=['cacc'])
                                for j in range(1, 4):
                                    P.op('dve', lambda e, c=c, j=j: e.scalar_tensor_tensor(
                                        out=cacc[:, c, :n], in0=XP[:, c, j:j + n], scalar=cw[:, c, j:j + 1], in1=cacc[:, c, :n],
                                        op0=ALU.mult, op1=ALU.add), r=['XP', 'cw', 'cacc'], w=['cacc'])
                            P.op('act', lambda e: e.activation(out=qkmT[:, :, :n], in_=cacc[:, :, :n], func=AF.Tanh),
                                 r=['cacc'], w=['qkmT'])
                            P.op('dve', lambda e: e.scalar_tensor_tensor(out=qkmT[:, :, :n], in0=qkmT[:, :, :n], scalar=1.0,
                                                                         in1=cacc[:, :, :n], op0=ALU.add, op1=ALU.mult),
                                 r=['cacc', 'qkmT'], w=['qkmT'])
                            if t == ntiles - 1:
                                cdst = (p_conv if prompt else s_conv)[si]
                                with nc.allow_non_contiguous_dma(reason="conv state"):
                                    for j in range(3):
                                        P.dma('sp', cdst[j, :].rearrange("(c p) -> p c", p=128), XP[:, :, n + j], r=['XP'])
                            else:
                                P.op('dve', lambda e: e.tensor_copy(out=XP[:, :, 0:3], in_=XP[:, :, n:n + 3]),
                                     r=['XP'], w=['XP'])

                        def seg_gates():
                            ps7 = psb[6]
                            for w2 in range(2):
                                for k in range(8):
                                    P.op('pe', lambda e, w2=w2, k=k: e.matmul(
                                        ps7[0:4, w2 * 128:w2 * 128 + n], lhsT=WT[:, k, 3584 + 4 * w2:3588 + 4 * w2],
                                        rhs=hnT[:, k, :n], start=(k == 0), stop=(k == 7)), r=['hnT'], w=['psb6'])
                            R_i, R_f, R_b, R_a, R_cm, R_mt, R_al, R_in, R_en, R_w = [rows[:, i, :n] for i in range(10)]
                            mprev = rsm[:, 0:1]
                            P.op('act', lambda e: e.activation(out=R_i, in_=ps7[0:4, 0:n], func=AF.Identity, bias=bif4[:, 0:1]),
                                 r=['psb6', 'bif4'], w=['rows'])
                            P.op('act', lambda e: e.activation(out=R_cm, in_=ps7[0:4, 128:128 + n], func=AF.Identity,
                                                               bias=bif4[:, 1:2]), r=['psb6', 'bif4'], w=['rows'])
                            P.op('act', lambda e: e.activation(out=R_in, in_=R_cm, func=AF.Abs), r=['rows'], w=['rows'])
                            P.op('act', lambda e: e.activation(out=R_in, in_=R_in, func=AF.Exp, scale=-1.0), r=['rows'], w=['rows'])
                            P.op('act', lambda e: e.activation(out=R_in, in_=R_in, func=AF.Ln, bias=1.0), r=['rows'], w=['rows'])
                            P.op('dve', lambda e: e.tensor_scalar(out=R_en, in0=R_cm, scalar1=0.0, scalar2=None, op0=ALU.min),
                                 r=['rows'], w=['rows'])
                            P.op('dve', lambda e: e.tensor_tensor(out=R_f, in0=R_en, in1=R_in, op=ALU.subtract),
                                 r=['rows'], w=['rows'])
                            P.op('dve', lambda e: e.tensor_tensor_scan(out=R_b, data0=onesf[0:4, :n], data1=R_f, initial=0.0,
                                                                       op0=ALU.mult, op1=ALU.add),
                                 r=['rows', 'onesf'], w=['rows'])
                            P.op('dve', lambda e: e.tensor_tensor(out=R_a, in0=R_i, in1=R_b, op=ALU.subtract),
                                 r=['rows'], w=['rows'])
                            P.op('dve', lambda e: e.tensor_tensor_scan(out=R_cm, data0=onesf[0:4, :n], data1=R_a, initial=-1e30,
                                                                       op0=ALU.mult, op1=ALU.max),
                                 r=['rows', 'onesf'], w=['rows'])
                            P.op('dve', lambda e: e.scalar_tensor_tensor(out=R_mt, in0=R_cm, scalar=mprev, in1=R_b,
                                                                         op0=ALU.max, op1=ALU.add),
                                 r=['rows', 'rsm'], w=['rows'])
                            P.op('dve', lambda e: e.tensor_tensor(out=R_al, in0=R_b, in1=R_mt, op=ALU.subtract),
                                 r=['rows'], w=['rows'])
                            P.op('act', lambda e: e.activation(out=R_in, in_=R_al, func=AF.Exp, bias=mprev),
                                 r=['rows', 'rsm'], w=['rows'])
                            P.op('act', lambda e: e.activation(out=R_en, in_=R_mt, func=AF.Exp, scale=-1.0),
                                 r=['rows'], w=['rows'])
                            P.op('dve', lambda e: e.tensor_tensor(out=rsm[:, 1:2], in0=rows[:, 2, n - 1:n], in1=rows[:, 5, n - 1:n],
                                                                  op=ALU.subtract), r=['rows'], w=['rsm'])
                            P.op('dve', lambda e: e.tensor_scalar(out=rsm[:, 3:4], in0=rsm[:, 1:2], scalar1=math.log(128.0 ** -0.5),
                                                                  scalar2=None, op0=ALU.add), r=['rsm'], w=['rsm'])
                            P.op('act', lambda e: e.activation(out=R_w, in_=R_a, func=AF.Exp, bias=rsm[:, 3:4]),
                                 r=['rows', 'rsm'], w=['rows'])
                            P.op('act', lambda e: e.activation(out=rsm[:, 2:3], in_=rsm[:, 0:1], func=AF.Exp, bias=rsm[:, 1:2]),
                                 r=['rsm'], w=['rsm'])
                            P.op('dve', lambda e: e.tensor_copy(out=rsm[:, 0:1], in_=rows[:, 5, n - 1:n]), r=['rows', 'rsm'], w=['rsm'])
                            for qi, ri in enumerate((3, 7, 8, 9, 6)):
                                P.op('dve', lambda e, qi=qi, ri=ri: e.tensor_copy(out=X36[0:4, qi, :n], in_=rows[:, ri, :n]),
                                     r=['rows'], w=['X36'])
                                P.op('dve', lambda e, qi=qi, ri=ri: e.tensor_tensor(out=X36[32:36, qi, :n], in0=rows[:, ri, :n],
                                                                                    in1=X36[0:4, qi, :n], op=ALU.subtract),
                                     r=['rows', 'X36'], w=['X36'])
                            P.op('dve', lambda e: e.tensor_copy(out=X36[0:4, 5, 0:1], in_=rsm[:, 2:3]), r=['rsm'], w=['X36'])
                            P.op('dve', lambda e: e.tensor_tensor(out=X36[32:36, 5, 0:1], in0=rsm[:, 2:3], in1=X36[0:4, 5, 0:1],
                                                                  op=ALU.subtract), r=['rsm', 'X36'], w=['X36'])
                            P.op('dve', lambda e: e.tensor_scalar(out=DG[0:4, :], in0=identb[0:4, 0:4], scalar1=X36[0:4, 5, 0:1],
                                                                  scalar2=None, op0=ALU.mult), r=['X36', 'identb'], w=['DG'])
                            P.op('dve', lambda e: e.tensor_scalar(out=DG[32:36, :], in0=identb[32:36, 32:36],
                                                                  scalar1=X36[32:36, 5, 0:1], scalar2=None, op0=ALU.mult),
                                 r=['X36', 'identb'], w=['DG'])
                            P.op('pe', lambda e: e.matmul(ps7[:, 256:260], lhsT=c36[:, 516:644], rhs=DG[:, :], start=True, stop=True),
                                 r=['c36', 'DG'], w=['psb6'])
                            P.op('act', lambda e: e.activation(out=dec_bc[:, :], in_=ps7[:, 256:260], func=AF.Copy),
                                 r=['psb6'], w=['dec_bc'])
                            for qi in range(4):
                                P.op('pe', lambda e, qi=qi: e.matmul(ps7[:n, 264 + qi * 4:268 + qi * 4], lhsT=X36[:, qi, :n],
                                                                     rhs=c36[:, 0:4], start=True, stop=True),
                                     r=['X36', 'c36'], w=['psb6'])
                            P.op('act', lambda e: e.activation(out=tok4[:n, :], in_=ps7[:n, 264:280], func=AF.Copy),
                                 r=['psb6'], w=['tok4'])

                        def seg_heads():
                            c_dk = 128.0 ** -0.5
                            HB = [(psb[h], f'psb{h}') for h in range(4)]
                            qms = [qkmT[:, h, :n] for h in range(4)]
                            kms = [qkmT[:, 4 + h, :n] for h in range(4)]
                            for h in range(4):
                                bk_, bkk = HB[h]
                                P.op('pe', lambda e, h=h, bk_=bk_: e.matmul(bk_[:n, 0:n], lhsT=kms[h], rhs=qms[h], start=True, stop=True),
                                     r=['qkmT'], w=[bkk])
                                P.op('pe', lambda e, h=h, bk_=bk_: e.matmul(bk_[:n, 128:128 + n], lhsT=c36[:, 4 + h * 128:4 + h * 128 + n],
                                                                            rhs=X36[:, 4, :n], start=True, stop=True),
                                     r=['c36', 'X36'], w=[bkk])
                            for h in range(4):
                                bk_, bkk = HB[h]
                                P.op('dve', lambda e, h=h, bk_=bk_: e.tensor_scalar(out=Dts[h][:n, :n], in0=bk_[:n, 128:128 + n],
                                                                                     scalar1=tok4[:n, h:h + 1], scalar2=0.0,
                                                                                     op0=ALU.add, op1=ALU.min),
                                     r=[bkk, 'tok4'], w=[f'Dt{h}'])
                            for h in range(4):
                                P.op('act', lambda e, h=h: e.activation(out=Dts[h][:n, :n], in_=Dts[h][:n, :n], func=AF.Exp),
                                     r=[f'Dt{h}'], w=[f'Dt{h}'])
                            for h in range(4):
                                P.op('pool', lambda e, h=h: e.tensor_tensor(out=Dts[h][:n, :n], in0=Dts[h][:n, :n],
                                                                            in1=mask[:n, 128:128 + n], op=ALU.mult),
                                     r=[f'Dt{h}', 'mask'], w=[f'Dt{h}'])
                            for h in range(4):
                                bk_, bkk = HB[h]
                                P.op('dve', lambda e, h=h, bk_=bk_: e.scalar_tensor_tensor(out=sqkTs[h][:n, :n], in0=bk_[:n, 0:n],
                                                                                            scalar=c_dk, in1=Dts[h][:n, :n],
                                                                                            op0=ALU.mult, op1=ALU.mult),
                                     r=[bkk, f'Dt{h}'], w=[f'sqkT{h}'])
                            for h in range(4):
                                bk_, bkk = HB[h]
                                P.op('pe', lambda e, h=h, bk_=bk_: e.matmul(bk_[:n, 0:129], lhsT=sqkTs[h][:n, :n], rhs=vb_aug[:n, h, :],
                                                                            start=True, stop=True), r=[f'sqkT{h}', 'vb_aug'], w=[bkk])
                                P.op('pe', lambda e, h=h, bk_=bk_: e.matmul(bk_[:n, 129:258], lhsT=qms[h], rhs=Cbf[:, h, :],
                                                                            start=True, stop=True), r=['qkmT', f'Cbf{h}'], w=[bkk])
                            for h in range(4):
                                bk_, bkk = HB[h]
                                P.op('act', lambda e, h=h, bk_=bk_: e.activation(out=numAs[h][:n, :], in_=bk_[:n, 0:129], func=AF.Copy),
                                     r=[bkk], w=[f'numA{h}'])
                            for h in range(4):
                                bk_, bkk = HB[h]
                                P.op('dve', lambda e, h=h, bk_=bk_: e.scalar_tensor_tensor(out=numAs[h][:n, :], in0=bk_[:n, 129:258],
                                                                                            scalar=tok4[:n, 4 + h:5 + h],
                                                                                            in1=numAs[h][:n, :],
                                                                                            op0=ALU.mult, op1=ALU.add),
                                     r=[bkk, 'tok4', f'numA{h}'], w=[f'numA{h}'])
                            for h in range(4):
                                P.op('act', lambda e, h=h: e.activation(out=sm1[:n, h, 5:6], in_=numAs[h][:n, 128:129], func=AF.Abs),
                                     r=[f'numA{h}'], w=[f'sm1{h}'])
                            for h in range(4):
                                P.op('dve', lambda e, h=h: e.tensor_scalar(out=sm1[:n, h, 0:1], in0=sm1[:n, h, 5:6],
                                                                           scalar1=tok4[:n, 8 + h:9 + h], scalar2=None,
                                                                           op0=ALU.max), r=[f'sm1{h}', 'tok4'], w=[f'sm1{h}'])
                                P.op('dve', lambda e, h=h: e.reciprocal(out=sm1[:n, h, 1:2], in_=sm1[:n, h, 0:1]),
                                     r=[f'sm1{h}'], w=[f'sm1{h}'])
                                P.op('dve', lambda e, h=h: e.scalar_tensor_tensor(out=hms[h][:n, :], in0=numAs[h][:n, 0:128],
                                                                                  scalar=sm1[:n, h, 1:2],
                                                                                  in1=sig_o[:n, h * 128:(h + 1) * 128],
                                                                                  op0=ALU.mult, op1=ALU.mult),
                                     r=[f'numA{h}', f'sm1{h}', 'sig_o'], w=[f'hm{h}'])
                            for h in range(4):
                                P.op('act', lambda e, h=h: e.activation(out=hsq[:n, :], in_=hms[h][:n, :], func=AF.Square,
                                                                        accum_out=sm1[:n, h, 2:3]), r=[f'hm{h}'], w=['hsq', f'sm1{h}'])
                            for h in range(4):
                                P.op('dve', lambda e, h=h: e.tensor_scalar(out=sm1[:n, h, 3:4], in0=sm1[:n, h, 2:3], scalar1=1.0 / 128,
                                                                           scalar2=EPS, op0=ALU.mult, op1=ALU.add),
                                     r=[f'sm1{h}'], w=[f'sm1{h}'])
                                P.op('pool', lambda e, h=h: e.tensor_tensor(out=sm1[:n, h, 4:5], in0=sm1[:n, h, 3:4],
                                                                            in1=negh[:n, 0:1], op=ALU.pow),
                                     r=[f'sm1{h}', 'negh'], w=[f'sm1{h}'])
                            for h in range(4):
                                P.op('dve', lambda e, h=h: e.scalar_tensor_tensor(out=yb_tok[:n, h * 128:(h + 1) * 128],
                                                                                  in0=hms[h][:n, :], scalar=sm1[:n, h, 4:5],
                                                                                  in1=gz[:n, h * 128:(h + 1) * 128],
                                                                                  op0=ALU.mult, op1=ALU.mult),
                                     r=[f'hm{h}', f'sm1{h}', 'gz'], w=['yb_tok'])
                            for h in range(4):
                                P.op('pe', lambda e, h=h: e.transpose(psT[:n, h * 128:(h + 1) * 128], kms[h], identb[:, :]),
                                     r=['qkmT', 'identb'], w=['psT'])
                            for h in range(4):
                                P.op('act', lambda e, h=h: e.activation(out=kws[h][:n, :], in_=psT[:n, h * 128:(h + 1) * 128], func=AF.Copy,
                                                                        scale=tok4[:n, 12 + h:13 + h]),
                                     r=['psT', 'tok4'], w=[f'kw{h}'])
                            for h in range(4):
                                bk_, bkk = HB[h]
                                P.op('pe', lambda e, h=h, bk_=bk_: e.matmul(bk_[:, 258:387], lhsT=kws[h][:n, :], rhs=vb_aug[:n, h, :],
                                                                            start=True, stop=True), r=[f'kw{h}', 'vb_aug'], w=[bkk])
                            for h in range(4):
                                bk_, bkk = HB[h]
                                P.op('dve', lambda e, h=h, bk_=bk_: e.scalar_tensor_tensor(out=Caug[:, h, :], in0=Caug[:, h, :],
                                                                                            scalar=dec_bc[:, h:h + 1], in1=bk_[:, 258:387],
                                                                                            op0=ALU.mult, op1=ALU.add),
                                     r=['Caug', 'dec_bc', bkk], w=['Caug'])
                            for h in range(4):
                                P.op('act', lambda e, h=h: e.activation(out=Cbf[:, h, :], in_=Caug[:, h, :], func=AF.Copy),
                                     r=['Caug'], w=[f'Cbf{h}'])

                        def seg_yb():
                            for h in range(4):
                                P.op('pe', lambda e, h=h: e.transpose(psT[:, (4 + h) * 128:(4 + h) * 128 + n],
                                                                      yb_tok[:n, h * 128:(h + 1) * 128], identb[:n, :n]),
                                     r=['yb_tok', 'identb'], w=['psT'])
                            pv = psT[:, :].rearrange("p (k c) -> p k c", c=128)
                            P.op('act', lambda e: e.activation(out=ymixT[:, 4:8, tok0:tok0 + n], in_=pv[:, 4:8, :n], func=AF.Copy),
                                 r=['psT'], w=['ymixT'])

                        if t == 0:
                            seg_norm(0)
                            seg_fproj()
                        P.merged([seg_tproj, seg_conv, seg_gates])
                        segs = [lambda: (seg_rope(), seg_kv()), seg_heads]
                        if t + 1 < ntiles_run:
                            segs.append(lambda: (seg_norm(t + 1), seg_fproj()))
                        P.merged(segs)
                        seg_yb()

                    Cd, nd, md = (p_C, p_n, p_m) if prompt else (s_C, s_n, s_m)
                    for h in range(4):
                        P.dma('sp', Cd[si, h], Caug[:, h, 0:128], r=['Caug'])
                    with nc.allow_non_contiguous_dma(reason="small state"):
                        P.dma('sp', nd[si].rearrange("h d -> d h"), Caug[:, :, 128], r=['Caug'])
                        P.dma('sp', md[si].rearrange("(h o) -> h o", o=1), rsm[:, 0:1], r=['rsm'])

                    if prompt:
                        P.dma('sp', scr_n2[si], n2max[:, :], r=['n2max'], w=['scr_n2'])
                        with nc.allow_non_contiguous_dma(reason="tiny transpose"):
                            P.dma('sp', Dt[0:16, :], scr_n2[si].rearrange("p j -> j p"), r=['scr_n2'], w=['Dt0'])
                        P.op('dve', lambda e: e.tensor_reduce(out=sm1[0:16, 0, 0:1], in_=Dt[0:16, :], axis=AX.X, op=ALU.max),
                             r=['Dt0'], w=['sm10'])
                        with nc.allow_non_contiguous_dma(reason="tiny"):
                            P.dma('sp', scr_mx[si].rearrange("(j o) -> j o", o=1), sm1[0:16, 0, 0:1], r=['sm10'], w=['scr_mx'])
                        P.dma('sp', n2t[:, :], scr_mx[si].partition_broadcast(128), r=['scr_mx'], w=['n2t'])
                        P.op('dve', lambda e: e.tensor_tensor(out=negB[:, :], in0=n2t[:, 0:8], in1=n2t[:, 8:16], op=ALU.mult),
                             r=['n2t'], w=['negB'])
                        P.op('act', lambda e: e.activation(out=negB[:, :], in_=negB[:, :], func=AF.Sqrt), r=['negB'], w=['negB'])
                        P.op('dve', lambda e: e.tensor_scalar(out=negB[:, :], in0=negB[:, :], scalar1=-0.125, scalar2=None,
                                                              op0=ALU.mult), r=['negB'], w=['negB'])
                    P.barrier()

                if shared_wo is None:
                    WO = sb("WO", [128, 8, 1024], BF16, qs)
                    wos = [sb(f"wos{i}", [128, 1024], F32, qs) for i in range(2)]
                else:
                    WO = shared_wo

                def load_wo():
                    for k in range(8):
                        P.dma('sp', wos[k % 2][:, :], w_out[k * 128:(k + 1) * 128, :], w=[f'wos{k % 2}'])
                        P.op('act', lambda e, k=k: e.activation(out=WO[:, k, 0:512], in_=wos[k % 2][:, 0:512], func=AF.Copy),
                             r=[f'wos{k % 2}'], w=['WO'])
                        P.op('dve', lambda e, k=k: e.tensor_copy(out=WO[:, k, 512:1024], in_=wos[k % 2][:, 512:1024]),
                             r=[f'wos{k % 2}'], w=['WO'])
                if prompt:
                    load_wo()

                if not prompt:
                    with ExitStack() as sa:
                        Kg = [sb(f"Kg{i}", [128, 512], F32, sa) for i in range(2)]
                        Vs = sb("Vs", [1, 4, 512], F32, sa)
                        qbc = [sb(f"qbc{i}", [128, 512], F32, sa) for i in range(2)]
                        prod = sb("prod", [128, 512], F32, sa)
                        Skh = sb("Skh", [128, 4, 32], F32, sa)
                        STs = sb("STs", [32, 388], F32, sa)
                        Pn = sb("Pn", [32, 388], F32, sa)
                        stat = sb("stat", [32, 8], F32, sa)
                        Pk = sb("Pk", [128, 4, 32], F32, sa)
                        msk = sb("msk", [8, 512], F32, sa)
                        attn = sb("attn", [4, 512], F32, sa)
                        ya = sb("ya", [4, 512], BF16, sa)
                        b = si
                        kj = 0
                        for t in range(TS):
                            qb_ = qbc[t % 2]
                            qk_ = f'qbc{t % 2}'
                            P.dma('sp', qb_[:, :], qs_scr[b, t, :].partition_broadcast(128), r=['qs_scr'], w=[qk_])
                            srcs = [
                                (s_k[b, 1916 + t:1916 + t + 128, :], s_v[b, 1916 + t:1916 + t + 128, :], 128, True),
                                (cache_k[b, slice(1536 + t, 1536 + t + 509, 4), :], cache_v[b, slice(1536 + t, 1536 + t + 509, 4), :], 128, False),
                                (cache_k[b, slice(t, t + 16 * 127 + 1, 16), :], cache_v[b, slice(t, t + 16 * 127 + 1, 16), :], 128, False),
                                (s_k[b, WB - TS + t:WB - TS + t + 1, :], s_v[b, WB - TS + t:WB - TS + t + 1, :], 1, True),
                            ]
                            for beta, (ksrc, vsrc, np_, from_out) in enumerate(srcs):
                                if beta in (1, 2):
                                    kg = KgC[:, t * 2 + beta - 1, :]
                                    kk = f'KgC{t * 2 + beta - 1}'
                                else:
                                    kg = Kg[kj % 2]
                                    kk = f'Kg{kj % 2}'
                                    kj += 1
                                    P.dma('sp', kg[0:np_, :], ksrc, r=[f's_kn{si}'] + ([f's_kc{si}'] if beta == 0 else []), w=[kk])
                                    if beta == 0:
                                        P.dma('sp', Vg[:, t * 3, :], vsrc, r=[f's_vn{si}', f's_vc{si}'], w=[f'Vg{t * 3}'])
                                    else:
                                        P.dma('sp', Vs[0:1, t, :], vsrc, r=[f's_vn{si}'], w=['Vs'])
                                P.op('dve', lambda e, kg=kg, np_=np_, qb_=qb_: e.tensor_tensor(
                                    out=prod[0:np_, :], in0=kg[0:np_, :], in1=qb_[0:np_, :], op=ALU.mult),
                                    r=[kk, qk_], w=['prod'])
                                P.op('dve', lambda e, np_=np_, beta=beta, t=t: e.tensor_reduce(
                                    out=Skh[0:np_, beta, t * 8:(t + 1) * 8],
                                    in_=prod[0:np_, :].rearrange("p (h d) -> p h d", d=64), axis=AX.X, op=ALU.add),
                                    r=['prod'], w=['Skh'])
                        psS, psP, psO, psA = psb[0], psb[1], psb[2], psb[3]
                        for beta in range(3):
                            P.op('pe', lambda e, beta=beta: e.transpose(psS[0:32, beta * 128:(beta + 1) * 128], Skh[:, beta, :],
                                                                        identf[:, :]), r=['Skh', 'identf'], w=['psb0'])
                        P.op('pe', lambda e: e.transpose(psS[0:32, 384:385], Skh[0:1, 3, :], identf[0:1, 0:1]),
                             r=['Skh', 'identf'], w=['psb0'])
                        P.op('act', lambda e: e.activation(out=STs[:, 0:385], in_=psS[0:32, 0:385], func=AF.Copy),
                             r=['psb0'], w=['STs'])
                        P.op('dve', lambda e: e.tensor_reduce(out=stat[:, 0:1], in_=STs[:, 0:385], axis=AX.X, op=ALU.max),
                             r=['STs'], w=['stat'])
                        P.op('dve', lambda e: e.tensor_scalar(out=stat[:, 1:2], in0=stat[:, 0:1], scalar1=-0.125, scalar2=None,
                                                              op0=ALU.mult), r=['stat'], w=['stat'])
                        P.op('act', lambda e: e.activation(out=Pn[:, 0:385], in_=STs[:, 0:385], func=AF.Exp, scale=0.125,
                                                           bias=stat[:, 1:2]), r=['STs', 'stat'], w=['Pn'])
                        P.op('dve', lambda e: e.tensor_scalar(out=Pn[:, 384:385], in0=Pn[:, 384:385], scalar1=3.0, scalar2=None,
                                                              op0=ALU.mult), r=['Pn'], w=['Pn'])
                        P.op('dve', lambda e: e.tensor_reduce(out=stat[:, 2:3], in_=Pn[:, 0:385], axis=AX.X, op=ALU.add),
                             r=['Pn'], w=['stat'])
                        P.op('dve', lambda e: e.reciprocal(out=stat[:, 3:4], in_=stat[:, 2:3]), r=['stat'], w=['stat'])
                        P.op('dve', lambda e: e.tensor_scalar(out=Pn[:, 0:385], in0=Pn[:, 0:385], scalar1=stat[:, 3:4],
                                                              scalar2=None, op0=ALU.mult), r=['Pn', 'stat'], w=['Pn'])
                        for beta in range(3):
                            P.op('pe', lambda e, beta=beta: e.transpose(psP[:, beta * 32:(beta + 1) * 32],
                                                                        Pn[:, beta * 128:(beta + 1) * 128], identf[0:32, 0:32]),
                                 r=['Pn', 'identf'], w=['psb1'])
                        P.op('pe', lambda e: e.transpose(psP[0:1, 96:128], Pn[:, 384:385], identf[0:32, 0:32]),
                             r=['Pn', 'identf'], w=['psb1'])
                        P.op('act', lambda e: e.activation(out=Pk[:, 0:3, :], in_=psP[:, 0:96].rearrange("p (b x) -> p b x", x=32),
                                                           func=AF.Copy), r=['psb1'], w=['Pk'])
                        P.op('act', lambda e: e.activation(out=Pk[0:1, 3, :], in_=psP[0:1, 96:128], func=AF.Copy),
                             r=['psb1'], w=['Pk'])
                        for t in range(TS):
                            for beta in range(3):
                                P.op('pe', lambda e, t=t, beta=beta: e.matmul(
                                    psO[0:8, :], lhsT=Pk[:, beta, t * 8:(t + 1) * 8], rhs=Vg[:, t * 3 + beta, :],
                                    start=(beta == 0), stop=False), r=['Pk', f'Vg{t * 3 + beta}'], w=['psb2'])
                            P.op('pe', lambda e, t=t: e.matmul(psO[0:8, :], lhsT=Pk[0:1, 3, t * 8:(t + 1) * 8], rhs=Vs[0:1, t, :],
                                                               start=False, stop=True), r=['Pk', 'Vs'], w=['psb2'])
                            P.op('dve', lambda e: e.tensor_tensor(out=msk[:, :], in0=psO[0:8, :], in1=bdc[:, 0:512], op=ALU.mult),
                                 r=['psb2', 'bdc'], w=['msk'])
                            P.op('pe', lambda e, t=t: e.matmul(psA[0:4, :], lhsT=bdc[:, 512 + 4 * t:516 + 4 * t], rhs=msk[:, :],
                                                               start=(t == 0), stop=(t == TS - 1)), r=['msk', 'bdc'], w=['psb3'])
                        P.op('act', lambda e: e.activation(out=attn[:, :], in_=psA[0:4, :], func=AF.Copy), r=['psb3'], w=['attn'])
                        P.op('dve', lambda e: e.tensor_tensor(out=ya[:, :], in0=attn[:, :], in1=za_s[0:4, :], op=ALU.mult),
                             r=['attn', 'za_tok'], w=['ya'])
                        for pr in range(4):
                            P.op('pe', lambda e, pr=pr: e.transpose(psT[:, pr * 128:pr * 128 + TS], ya[:, pr * 128:(pr + 1) * 128],
                                                                    identb[0:TS, 0:TS]), r=['ya', 'identb'], w=['psT'])
                        pvs = psT[:, :].rearrange("p (k c) -> p k c", c=128)
                        P.op('act', lambda e: e.activation(out=ymixT[:, 0:4, 0:TS], in_=pvs[:, 0:4, 0:TS], func=AF.Copy),
                             r=['psT'], w=['ymixT'])
                        P.barrier()

                if prompt and KSTAGE >= 2:
                    with ExitStack() as as_:
                        NUM = sb("NUM", [128, SEQ], F32, as_)
                        DEN = sb("DEN", [128, SEQ], F32, as_)
                        Vbufs = [sb(f"Vbuf{i}", [128, 16, 256], BF16, as_) for i in range(2)]
                        Pts = [sb(f"Pt{i}", [128, 2, 2, 128], BF16, as_) for i in range(2)]
                        mk3 = mask[:, :].rearrange("p (b x) -> p b x", x=128)
                        groups = [(pr, dil) for pr in range(4) for dil in (1, 4, 16)]

                        def load_v(gi):
                            pr, dil = groups[gi]
                            ncb = 16 // dil
                            vb_ = Vbufs[gi % 2]
                            for r_ in range(dil):
                                src = vaug_scr[si, :, pr * 256:(pr + 1) * 256].rearrange(
                                    "(c i d) x -> d i c x", d=dil, i=128)[r_]
                                P.dma('sp', vb_[:, r_ * ncb:(r_ + 1) * ncb, :], src, r=['vaug_scr'], w=[f'Vbuf{gi % 2}'])

                        def stageA(U):
                            pr, u, qsl, psl, hasp = U['pr'], U['u'], U['qsl'], U['psl'], U['hasp']
                            Pt = Pts[u]
                            b0 = 0 if hasp else 1
                            for e2 in range(2):
                                lo, hi = e2 * 64, e2 * 64 + 64
                                Sb = psb[2 * u + e2]
                                sk = f'psb{2 * u + e2}'
                                if hasp:
                                    P.op('pe', lambda e: e.matmul(Sb[:, 0:128], lhsT=kT[lo:hi, pr, psl], rhs=qT[lo:hi, pr, qsl],
                                                                  start=True, stop=True), r=['qT', 'kT'], w=[sk])
                                P.op('pe', lambda e: e.matmul(Sb[:, 128:256], lhsT=kT[lo:hi, pr, qsl], rhs=qT[lo:hi, pr, qsl],
                                                              start=True, stop=True), r=['qT', 'kT'], w=[sk])
                            for e2 in range(2):
                                hh = 2 * pr + e2
                                Sb = psb[2 * u + e2]
                                sk = f'psb{2 * u + e2}'
                                P.op('act', lambda e: e.activation(
                                    out=Pt[:, e2, b0:2, :], in_=Sb[:, b0 * 128:256].rearrange("p (b x) -> p b x", x=128),
                                    func=AF.Exp, scale=0.125, bias=negB[:, hh:hh + 1]), r=[sk, 'negB'], w=[f'Pt{u}_{e2}'])
                            for e2 in range(2):
                                eng = 'pool'
                                P.op(eng, lambda e: e.tensor_tensor(out=Pt[:, e2, b0:2, :], in0=Pt[:, e2, b0:2, :],
                                                                    in1=mk3[:, b0:2, :], op=ALU.mult),
                                     r=[f'Pt{u}_{e2}', 'mask'], w=[f'Pt{u}_{e2}'])

                        def stageB(U):
                            u, qsl, hasp, blk, gi = U['u'], U['qsl'], U['hasp'], U['blk'], U['gi']
                            Pt = Pts[u]
                            vb_ = Vbufs[gi % 2]
                            vk = f'Vbuf{gi % 2}'
                            Ob = psb[4 + u]
                            ok = f'psb{4 + u}'
                            b0 = 0 if hasp else 1
                            for (e2, r0, c0, l0) in ((0, 0, 0, 0), (1, 64, 0, 192), (0, 0, 128, 64), (1, 64, 128, 128)):
                                for bb in range(b0, 2):
                                    vbi = blk - 1 if bb == 0 else blk
                                    P.op('pe', lambda e: e.matmul(Ob[r0:r0 + 64, c0:c0 + 128],
                                                                  lhsT=vb_[:, vbi, l0:l0 + 64], rhs=Pt[:, e2, bb, :],
                                                                  start=(bb == b0), stop=(bb == 1)),
                                         r=[vk, f'Pt{u}_{e2}'], w=[ok])
                            P.op('dve', lambda e: e.tensor_tensor(out=NUM[:, qsl], in0=NUM[:, qsl], in1=Ob[:, 0:128],
                                                                  op=ALU.add), r=[ok, 'NUM'], w=['NUM'])
                            P.op('dve', lambda e: e.tensor_tensor(out=DEN[:, qsl], in0=DEN[:, qsl], in1=Ob[:, 128:256],
                                                                  op=ALU.add), r=[ok, 'DEN'], w=['DEN'])

                        ui = 0
                        load_v(0)
                        for gi, (pr, dil) in enumerate(groups):
                            if gi + 1 < len(groups):
                                load_v(gi + 1)
                            if dil == 1:
                                P.op('dve', lambda e: e.memset(NUM[:, :], 0.0), w=['NUM'])
                                P.op('pool', lambda e: e.memset(DEN[:, :], 0.0), w=['DEN'])
                            ncb = 16 // dil
                            units = []
                            for r_ in range(dil):
                                for c in range(ncb):
                                    t0 = r_ + dil * 128 * c
                                    units.append(dict(pr=pr, gi=gi, blk=r_ * ncb + c, u=ui % 2, hasp=(c > 0),
                                                      qsl=slice(t0, t0 + dil * 127 + 1, dil),
                                                      psl=slice(t0 - dil * 128, t0 - dil * 128 + dil * 127 + 1, dil)))
                                    ui += 1
                            stageA(units[0])
                            for i in range(len(units)):
                                if i + 1 < len(units):
                                    stageA(units[i + 1])
                                stageB(units[i])
                            if dil == 16:
                                P.op('dve', lambda e: e.reciprocal(out=DEN[:, :], in_=DEN[:, :]), r=['DEN'], w=['DEN'])
                                P.op('dve', lambda e: e.tensor_tensor(out=NUM[:, :], in0=NUM[:, :], in1=DEN[:, :], op=ALU.mult),
                                     r=['NUM', 'DEN'], w=['NUM'])
                                P.op('dve', lambda e: e.tensor_tensor(out=ymixT[:, pr, :], in0=NUM[:, :], in1=ymixT[:, pr, :],
                                                                      op=ALU.mult), r=['NUM', 'ymixT'], w=['ymixT'])
                        P.barrier()

                with ExitStack() as os_:
                    xts = [sb(f"xo{i}", [128, D], F32, os_) for i in range(2)]
                    yts = [sb(f"yo{i}", [128, D], F32, os_) for i in range(2)]
                    junk = sb("junk", [128, D], BF16, os_)
                    so = sb("so", [128, 4], F32, os_)
                    if not prompt and shared_wo is None:
                        load_wo()

                    def load_x2(t):
                        P.dma('sp', xts[t % 2][:n, :], xsrc[t * n:(t + 1) * n, :], w=[f'xo{t % 2}'])
                    load_x2(0)
                    for t in range(ntiles_run if KSTAGE >= 3 else 0):
                        if t + 1 < ntiles:
                            load_x2(t + 1)
                        xt = xts[t % 2]
                        xk = f'xo{t % 2}'
                        yt = yts[t % 2]
                        yk = f'yo{t % 2}'
                        tok0 = t * n
                        for c in range(2):
                            bank = psb[4 + c]
                            bk = f'psb{4 + c}'
                            for k in range(8):
                                P.op('pe', lambda e, c=c, k=k, bank=bank: e.matmul(
                                    bank[:n, :], lhsT=ymixT[:, k, tok0:tok0 + n], rhs=WO[:, k, c * 512:(c + 1) * 512],
                                    start=(k == 0), stop=(k == 7)), r=['ymixT', 'WO'], w=[bk])
                            P.op('dve', lambda e, c=c, bank=bank: e.tensor_tensor(
                                out=yt[:n, c * 512:(c + 1) * 512], in0=bank[:n, :], in1=xt[:n, c * 512:(c + 1) * 512], op=ALU.add),
                                r=[bk, xk], w=[yk])
                        P.op('act', lambda e: e.activation(out=junk[:n, :], in_=yt[:n, :], func=AF.Square, accum_out=so[:n, 0:1]),
                             r=[yk], w=['junk', 'so'])
                        P.op('dve', lambda e: e.tensor_scalar(out=so[:n, 1:2], in0=so[:n, 0:1], scalar1=1.0 / D, scalar2=EPS,
                                                              op0=ALU.mult, op1=ALU.add), r=['so'], w=['so'])
                        P.op('pool', lambda e: e.tensor_tensor(out=so[:n, 2:3], in0=so[:n, 1:2], in1=negh[:n, 0:1], op=ALU.pow),
                             r=['so', 'negh'], w=['so'])
                        P.op('dve', lambda e: e.scalar_tensor_tensor(out=yt[:n, :], in0=yt[:n, :], scalar=so[:n, 2:3],
                                                                     in1=gf_bc[:n, :], op0=ALU.mult, op1=ALU.mult),
                             r=[yk, 'so', 'gf_bc'], w=[yk])
                        P.dma('sp', ydst[tok0:tok0 + n, :], yt[:n, :], r=[yk])
                    P.barrier()

        if not KNOSAMPLE:
            with ExitStack() as sws:
                WOs = sb("WOs", [128, 8, 1024], BF16, sws)
                wst2 = [sb(f"wst2_{i}", [128, 1024], F32, sws) for i in range(1)]
                for k in range(8):
                    P.dma('sp', wst2[0][:, :], w_out[k * 128:(k + 1) * 128, :], w=['wst2_0'])
                    P.op('act', lambda e, k=k: e.activation(out=WOs[:, k, 0:512], in_=wst2[0][:, 0:512], func=AF.Copy),
                         r=['wst2_0'], w=['WO'])
                    P.op('dve', lambda e, k=k: e.tensor_copy(out=WOs[:, k, 512:1024], in_=wst2[0][:, 512:1024]),
                         r=['wst2_0'], w=['WO'])
                def shift_copy(b):
                    P.dma('sp', s_k[b, 0:WB - TS, :], cache_k[b, TS:WB, :], w=[f's_kc{b}'])
                    P.dma('sp', s_v[b, 0:WB - TS, :], cache_v[b, TS:WB, :], w=[f's_vc{b}'])
                shift_copy(0)
                for b in range(NSB):
                    if b + 1 < NSB:
                        shift_copy(b + 1)
                    if b < min(NSB, KSB):
                        run_seq('s', b, shared_wo=WOs)
                P.barrier()
        for si in range(min(NSEQ, KSEQS)):
            run_seq('p', si)
        P.barrier()
    return nc


_CACHE = {}


def _consts():
    identb = np.eye(128, dtype=np.float32).astype(ml_dtypes.bfloat16)
    identf = np.eye(128, dtype=np.float32)
    p = np.arange(128)[:, None]
    f = np.arange(128)[None, :]
    mask = np.concatenate([(p >= f), (p <= f)], axis=1).astype(np.float32).astype(ml_dtypes.bfloat16)
    sel = np.zeros((4, 4, 128), np.float32)
    for h in range(4):
        sel[h, h, :] = 1.0
    sel = sel.reshape(4, 512)
    half = 8
    inv = (np.float32(500000.0) ** (-np.arange(half, dtype=np.float32) * np.float32(2.0) / np.float32(16))).astype(np.float32)

    def tabs(pos):
        ang = pos.astype(np.float32)[:, None] * inv[None, :]
        c = np.cos(ang).astype(np.float32)
        s = np.sin(ang).astype(np.float32)
        return np.tile(c, (1, 8)), np.tile(s, (1, 8))
    cp, sp = tabs(np.arange(SEQ))
    cs, ss = tabs(PAST + np.arange(TS))
    bd = np.zeros((8, 528), np.float32)
    for h in range(8):
        bd[h, h * 64:(h + 1) * 64] = 1.0
    for t in range(4):
        bd[:, 512 + 4 * t + t] = 1.0
    c36 = np.zeros((36, 644), np.float32)
    for h in range(4):
        for base in (0, 32):
            c36[base + h, h] = 1.0
            c36[base + h, 4 + h * 128:4 + (h + 1) * 128] = 1.0
            c36[base + h, 516:644] = 1.0
    c36 = c36.astype(ml_dtypes.bfloat16)
    return dict(c_identb=identb, c_identf=identf, c_mask=mask, c_sel=sel, c_36=c36, c_bd=bd,
                c_cosp=np.ascontiguousarray(cp), c_sinp=np.ascontiguousarray(sp),
                c_coss=np.ascontiguousarray(cs), c_sins=np.ascontiguousarray(ss))


def kernel(x_prompt, x_sample, cache_win_k, cache_win_v, state_conv, state_C, state_n, state_m,
           norm_g, w_in, conv_w, conv_b, b_i, b_f, mlstm_norm_g, w_out, final_norm_g):
    f = lambda a: np.ascontiguousarray(np.asarray(a, dtype=np.float32))
    if 'nc' not in _CACHE:
        _CACHE['nc'] = build_program()
    nc = _CACHE['nc']
    consts = _consts()
    shared = dict(norm_g=f(norm_g[0]), w_in=f(w_in[0]), conv_w=f(conv_w[0]), conv_b=f(conv_b[0]),
                  b_if=f(np.concatenate([np.asarray(b_i[0]), np.asarray(b_f[0])])), gm=f(mlstm_norm_g[0]),
                  w_out=f(w_out[0]), gf=f(final_norm_g))
    shared.update(consts)
    in_maps = []
    for c in range(NCORES):
        m = dict(shared)
        m['xp'] = f(x_prompt[c * NSEQ:(c + 1) * NSEQ])
        m['xs'] = f(x_sample[c * NSB:(c + 1) * NSB])
        WBD = 8 if KNOSAMPLE else WB
        m['cache_k'] = f(cache_win_k[0, c * NSB:(c + 1) * NSB, :WBD]).reshape(NSB, WBD, 512)
        m['cache_v'] = f(cache_win_v[0, c * NSB:(c + 1) * NSB, :WBD]).reshape(NSB, WBD, 512)
        m['st_conv'] = f(state_conv[0, c * NSB:(c + 1) * NSB])
        m['st_C'] = f(state_C[0, c * NSB:(c + 1) * NSB])
        m['st_n'] = f(state_n[0, c * NSB:(c + 1) * NSB])
        m['st_m'] = f(state_m[0, c * NSB:(c + 1) * NSB])
        in_maps.append(m)
    if KONECORE:
        res = run_bass_kernel_spmd(nc, in_maps[:1], core_ids=[0])
        R = [res.results[0]] * NCORES
    else:
        res = run_bass_kernel_spmd(nc, in_maps, core_ids=list(range(NCORES)))
        R = res.results
    cat = lambda name: np.concatenate([np.asarray(r[name]) for r in R], axis=0)
    y_prompt = cat('y_p')
    y_sample = cat('y_s')
    p_k = cat('p_k').reshape(1, 16, SEQ, 8, 64)
    p_v = cat('p_v').reshape(1, 16, SEQ, 8, 64)
    p_conv = cat('p_conv')[None]
    p_C = cat('p_C')[None]
    p_n = cat('p_n')[None]
    p_m = cat('p_m')[None]
    if KNOSAMPLE:
        s_k = np.zeros((1, 32, WB, 8, 64), np.float32)
        s_v = np.zeros((1, 32, WB, 8, 64), np.float32)
    else:
        s_k = cat('s_k').reshape(1, 32, WB, 8, 64)
        s_v = cat('s_v').reshape(1, 32, WB, 8, 64)
    s_conv = cat('s_conv')[None]
    s_C = cat('s_C')[None]
    s_n = cat('s_n')[None]
    s_m = cat('s_m')[None]
    return (y_prompt, y_sample, p_k, p_v, p_conv, p_C, p_n, p_m, s_k, s_v, s_conv, s_C, s_n, s_m)
```
